# Optimizing a Trainium2 kernel written in Bass

```python
import math
import jax, jax.numpy as jnp
from jax import lax
import numpy as np

D_MODEL = 2048
BATCH = 1
SEQ = 8192
DEPTH = 1

MEM_LEN = 256
HEAD_DIM = 128
GLA_HEADS = 4
GLA_DK = 64
GLA_DV = 128
GLA_LOWRANK = 16
GLA_TAU = 16.0
GLA_CHUNK = 64
NSA_HEADS = 8
NSA_KV_HEADS = 2
NSA_DK = HEAD_DIM
CMP_LEN = 32
CMP_STRIDE = 16
CMP_HIDDEN = 256
SEL_LEN = 64
SEL_TOPK = 16
WINDOW = 512
MEM_HEADS = 4
MEM_DK = HEAD_DIM
D_MIX = GLA_HEADS * GLA_DV + NSA_HEADS * NSA_DK + MEM_HEADS * MEM_DK
D_FF = 5632
MACARON_W = 0.5
QBLK = 128
ROPE_THETA = 10000.0
EPS = 1e-6
NEG_INF = -1e30
TINY = 1e-30
FORCE_SCORE = 1e4

IN_SIZES = (GLA_HEADS * GLA_DK, GLA_HEADS * GLA_DK, GLA_HEADS * GLA_DV, GLA_HEADS * GLA_DV, GLA_LOWRANK,
            NSA_HEADS * NSA_DK) + (NSA_KV_HEADS * NSA_DK,) * 6 + (NSA_HEADS * 3, MEM_HEADS * MEM_DK)
D_IN = sum(IN_SIZES)

kernel_name = 'hybrid_gla_nsa_memx_macaron'


def rms_norm(x, g):
    xf = x.astype(jnp.float32)
    y = xf * lax.rsqrt(jnp.mean(xf * xf, axis=-1, keepdims=True) + EPS)
    return (y * g.astype(jnp.float32)).astype(x.dtype)


def rope(x, pos):
    half = x.shape[-1] // 2
    inv = ROPE_THETA ** (-jnp.arange(half, dtype=jnp.float32) / half)
    ang = pos.astype(jnp.float32)[:, None, :, None] * inv
    cos, sin = jnp.cos(ang), jnp.sin(ang)
    x1 = x[..., :half].astype(jnp.float32)
    x2 = x[..., half:].astype(jnp.float32)
    return jnp.concatenate([x1 * cos - x2 * sin, x2 * cos + x1 * sin], axis=-1).astype(x.dtype)


def heads(t, h):
    b, s, _ = t.shape
    return t.reshape(b, s, h, -1).transpose(0, 2, 1, 3)


def merge(t):
    b, h, s, d = t.shape
    return t.transpose(0, 2, 1, 3).reshape(b, s, h * d)


def masked_softmax(s, valid):
    s = jnp.where(valid, s, NEG_INF)
    e = jnp.exp(s - jnp.max(s, axis=-1, keepdims=True)) * valid
    return e / jnp.maximum(jnp.sum(e, axis=-1, keepdims=True), TINY)


def swiglu(x, w_gate, w_up, w_down):
    return (jax.nn.silu(x @ w_gate) * (x @ w_up)) @ w_down


def gla_chunked(q, k, v, log_a):
    b_, h_, s_, dk = q.shape
    dv = v.shape[-1]
    c = GLA_CHUNK
    n = s_ // c
    scale = dk ** -0.5
    causal = jnp.tril(jnp.ones((c, c), dtype=bool))

    def chunks(t):
        return jnp.moveaxis(t.reshape(b_, h_, n, c, t.shape[-1]), 2, 0)

    def step(state, inp):
        qc, kc, vc, lac = inp
        qc = qc.astype(jnp.float32) * scale
        kc = kc.astype(jnp.float32)
        vc = vc.astype(jnp.float32)
        bcum = jnp.cumsum(lac, axis=2)
        blast = bcum[:, :, -1:, :]
        o_inter = jnp.einsum('bhcd,bhde->bhce', qc * jnp.exp(bcum), state)
        decay = jnp.exp(jnp.where(causal[:, :, None], bcum[:, :, :, None, :] - bcum[:, :, None, :, :], -jnp.inf))
        attn = jnp.einsum('bhid,bhjd,bhijd->bhij', qc, kc, decay)
        o = o_inter + jnp.einsum('bhij,bhje->bhie', attn, vc)
        state = state * jnp.exp(blast)[:, :, 0, :, None] + jnp.einsum('bhcd,bhce->bhde', kc * jnp.exp(blast - bcum), vc)
        return state, o

    state0 = jnp.zeros((b_, h_, dk, dv), jnp.float32)
    _, o = lax.scan(step, state0, (chunks(q), chunks(k), chunks(v), chunks(log_a)))
    return jnp.moveaxis(o, 0, 2).reshape(b_, h_, s_, dv)


def nsa_attention(q, kc_raw, vc_raw, ks, vs, kw, vw, gates, positions, k_norm,
                  cmp_pos_k, cmp_w1_k, cmp_w2_k, cmp_pos_v, cmp_w1_v, cmp_w2_v):
    b_, hq, s_, dk = q.shape
    g_ = NSA_KV_HEADS
    hpg = hq // g_
    scale = dk ** -0.5

    n_cmp = (s_ - CMP_LEN) // CMP_STRIDE + 1
    cmp_start = jnp.arange(n_cmp) * CMP_STRIDE
    cmp_last = cmp_start + CMP_LEN - 1
    blk_idx = cmp_start[:, None] + jnp.arange(CMP_LEN)[None, :]

    def compress(t, pos_emb, w1, w2):
        blocks = t[:, :, blk_idx, :] + pos_emb
        flat = blocks.reshape(b_, g_, n_cmp, CMP_LEN * dk)
        return jax.nn.silu(flat @ w1) @ w2

    k_cmp = rope(rms_norm(compress(kc_raw, cmp_pos_k, cmp_w1_k, cmp_w2_k), k_norm[0]), positions[:, cmp_last])
    v_cmp = compress(vc_raw, cmp_pos_v, cmp_w1_v, cmp_w2_v)

    n_sel = s_ // SEL_LEN
    topk = min(SEL_TOPK, n_sel)
    sel_start = jnp.arange(n_sel) * SEL_LEN
    sel_ids = jnp.arange(n_sel)
    overlap = jnp.clip(jnp.minimum(cmp_start[:, None] + CMP_LEN, sel_start[None, :] + SEL_LEN)
                       - jnp.maximum(cmp_start[:, None], sel_start[None, :]), 0).astype(jnp.float32) / CMP_STRIDE

    kw_pad = jnp.pad(kw, ((0, 0), (0, 0), (WINDOW, 0), (0, 0)))
    vw_pad = jnp.pad(vw, ((0, 0), (0, 0), (WINDOW, 0), (0, 0)))
    bidx = jnp.arange(b_)[:, None, None, None]
    gidx = jnp.arange(g_)[None, :, None, None]
    qg = q.reshape(b_, g_, hpg, s_, dk)

    def block(qb):
        t0 = qb * QBLK
        tq = t0 + jnp.arange(QBLK)
        qs = lax.dynamic_slice_in_dim(qg, t0, QBLK, axis=3)
        gs = jax.nn.sigmoid(lax.dynamic_slice_in_dim(gates, t0, QBLK, axis=3).astype(jnp.float32))

        s = jnp.einsum('bghqd,bgnd->bghqn', qs, k_cmp).astype(jnp.float32) * scale
        p_cmp = masked_softmax(s, cmp_last[None, :] <= tq[:, None])
        o_cmp = jnp.einsum('bghqn,bgnd->bghqd', p_cmp, v_cmp)

        imp = jnp.einsum('bghqn,nm->bgqm', p_cmp, overlap)
        cur = tq // SEL_LEN
        causal = sel_start[None, :] <= tq[:, None]
        forced = (sel_ids[None, :] == 0) | (sel_ids[None, :] == cur[:, None]) | (sel_ids[None, :] == cur[:, None] - 1)
        score = jnp.where(causal, jnp.where(forced, FORCE_SCORE, imp), -FORCE_SCORE)
        _, sel = lax.top_k(score, topk)
        tok = (sel[..., None] * SEL_LEN + jnp.arange(SEL_LEN)).reshape(b_, g_, QBLK, topk * SEL_LEN)
        k_sel = ks[bidx, gidx, tok]
        v_sel = vs[bidx, gidx, tok]
        s = jnp.einsum('bghqd,bgqtd->bghqt', qs, k_sel).astype(jnp.float32) * scale
        p = masked_softmax(s, (tok <= tq[:, None])[:, :, None])
        o_slc = jnp.einsum('bghqt,bgqtd->bghqd', p, v_sel)

        k_win = lax.dynamic_slice_in_dim(kw_pad, t0, WINDOW + QBLK, axis=2)
        v_win = lax.dynamic_slice_in_dim(vw_pad, t0, WINDOW + QBLK, axis=2)
        kpos = t0 - WINDOW + jnp.arange(WINDOW + QBLK)
        valid = (kpos[None, :] <= tq[:, None]) & (kpos[None, :] > tq[:, None] - WINDOW) & (kpos[None, :] >= 0)
        s = jnp.einsum('bghqd,bgkd->bghqk', qs, k_win).astype(jnp.float32) * scale
        p = masked_softmax(s, valid)
        o_win = jnp.einsum('bghqk,bgkd->bghqd', p, v_win)

        return gs[..., 0:1] * o_cmp + gs[..., 1:2] * o_slc + gs[..., 2:3] * o_win

    out = lax.map(block, jnp.arange(s_ // QBLK))
    return out.transpose(1, 0, 4, 2, 3, 5).reshape(b_, s_, hq * dk)


def memory_cross_attention(q, mem, mem_in_norm, w_mem_kv, mem_q_norm, mem_k_norm):
    qh = rms_norm(heads(q, MEM_HEADS), mem_q_norm)
    kv = rms_norm(mem, mem_in_norm) @ w_mem_kv
    k, v = jnp.split(kv, 2, axis=-1)
    kh = rms_norm(heads(k, MEM_HEADS), mem_k_norm)
    vh = heads(v, MEM_HEADS)
    s = jnp.einsum('bhsd,bhmd->bhsm', qh, kh).astype(jnp.float32) * (MEM_DK ** -0.5)
    p = jax.nn.softmax(s, axis=-1)
    return merge(jnp.einsum('bhsm,bhmd->bhsd', p, vh))


def hybrid_layer(x, mem, positions, ffn1_norm, ffn1_w_gate, ffn1_w_up, ffn1_w_down, mix_norm, w_in,
                 gla_w_a, gla_b_a, gla_o_norm, nsa_q_norm, nsa_k_norm, nsa_cmp_pos_k, nsa_cmp_w1_k,
                 nsa_cmp_w2_k, nsa_cmp_pos_v, nsa_cmp_w1_v, nsa_cmp_w2_v, mem_in_norm, w_mem_kv,
                 mem_q_norm, mem_k_norm, w_out, ffn2_norm, ffn2_w_gate, ffn2_w_up, ffn2_w_down, final_norm):
    dt = x.dtype
    x = x + MACARON_W * swiglu(rms_norm(x, ffn1_norm), ffn1_w_gate, ffn1_w_up, ffn1_w_down)

    h = rms_norm(x, mix_norm)
    proj = h @ w_in
    splits = [int(v) for v in np.cumsum(IN_SIZES)[:-1]]
    (g_q, g_k, g_v, g_r, g_a, n_q, n_kc, n_vc, n_ks, n_vs, n_kw, n_vw, n_g, m_q) = jnp.split(proj, splits, axis=-1)
    b_, s_, _ = x.shape

    log_a = jax.nn.log_sigmoid((g_a @ gla_w_a + gla_b_a).astype(jnp.float32)) / GLA_TAU
    o = gla_chunked(heads(g_q, GLA_HEADS), heads(g_k, GLA_HEADS), heads(g_v, GLA_HEADS), heads(log_a, GLA_HEADS))
    o_gla = (merge(rms_norm(o, gla_o_norm)) * jax.nn.silu(g_r.astype(jnp.float32))).astype(dt)

    q = rope(rms_norm(heads(n_q, NSA_HEADS), nsa_q_norm), positions)
    ks = rope(rms_norm(heads(n_ks, NSA_KV_HEADS), nsa_k_norm[1]), positions)
    kw = rope(rms_norm(heads(n_kw, NSA_KV_HEADS), nsa_k_norm[2]), positions)
    gates = n_g.reshape(b_, s_, NSA_KV_HEADS, NSA_HEADS // NSA_KV_HEADS, 3).transpose(0, 2, 3, 1, 4)
    o_nsa = nsa_attention(q, heads(n_kc, NSA_KV_HEADS), heads(n_vc, NSA_KV_HEADS), ks, heads(n_vs, NSA_KV_HEADS),
                          kw, heads(n_vw, NSA_KV_HEADS), gates, positions, nsa_k_norm,
                          nsa_cmp_pos_k, nsa_cmp_w1_k, nsa_cmp_w2_k, nsa_cmp_pos_v, nsa_cmp_w1_v, nsa_cmp_w2_v).astype(dt)

    o_mem = memory_cross_attention(m_q, mem, mem_in_norm, w_mem_kv, mem_q_norm, mem_k_norm).astype(dt)

    x = x + jnp.concatenate([o_gla, o_nsa, o_mem], axis=-1) @ w_out

    x = x + MACARON_W * swiglu(rms_norm(x, ffn2_norm), ffn2_w_gate, ffn2_w_up, ffn2_w_down)
    return rms_norm(x, final_norm)


def setup_inputs(seed: int = 0) -> dict:
    key = jax.random.key(seed)
    k = jax.random.split(key, 40)
    f32 = jnp.float32

    def w(kk, shape, fan_in):
        return jax.random.normal(kk, (DEPTH,) + shape, f32) * (fan_in ** -0.5)

    def gain(kk, shape):
        return 1.0 + 0.02 * jax.random.normal(kk, (DEPTH,) + shape, f32)

    def small(kk, shape, s):
        return s * jax.random.normal(kk, (DEPTH,) + shape, f32)

    return {
        'x': jax.random.normal(k[0], (BATCH, SEQ, D_MODEL), f32),
        'mem': jax.random.normal(k[1], (BATCH, MEM_LEN, D_MODEL), f32),
        'positions': jnp.broadcast_to(jnp.arange(SEQ, dtype=jnp.int32), (BATCH, SEQ)),
        'ffn1_norm': gain(k[2], (D_MODEL,)),
        'ffn1_w_gate': w(k[3], (D_MODEL, D_FF), D_MODEL),
        'ffn1_w_up': w(k[4], (D_MODEL, D_FF), D_MODEL),
        'ffn1_w_down': w(k[5], (D_FF, D_MODEL), D_FF),
        'mix_norm': gain(k[6], (D_MODEL,)),
        'w_in': w(k[7], (D_MODEL, D_IN), D_MODEL),
        'gla_w_a': w(k[8], (GLA_LOWRANK, GLA_HEADS * GLA_DK), GLA_LOWRANK),
        'gla_b_a': small(k[9], (GLA_HEADS * GLA_DK,), 0.1),
        'gla_o_norm': gain(k[10], (GLA_DV,)),
        'nsa_q_norm': gain(k[11], (NSA_DK,)),
        'nsa_k_norm': gain(k[12], (3, NSA_DK)),
        'nsa_cmp_pos_k': small(k[13], (CMP_LEN, NSA_DK), 0.1),
        'nsa_cmp_w1_k': w(k[14], (CMP_LEN * NSA_DK, CMP_HIDDEN), CMP_LEN * NSA_DK),
        'nsa_cmp_w2_k': w(k[15], (CMP_HIDDEN, NSA_DK), CMP_HIDDEN),
        'nsa_cmp_pos_v': small(k[16], (CMP_LEN, NSA_DK), 0.1),
        'nsa_cmp_w1_v': w(k[17], (CMP_LEN * NSA_DK, CMP_HIDDEN), CMP_LEN * NSA_DK),
        'nsa_cmp_w2_v': w(k[18], (CMP_HIDDEN, NSA_DK), CMP_HIDDEN),
        'mem_in_norm': gain(k[19], (D_MODEL,)),
        'w_mem_kv': w(k[20], (D_MODEL, 2 * MEM_HEADS * MEM_DK), D_MODEL),
        'mem_q_norm': gain(k[21], (MEM_DK,)),
        'mem_k_norm': gain(k[22], (MEM_DK,)),
        'w_out': w(k[23], (D_MIX, D_MODEL), D_MIX),
        'ffn2_norm': gain(k[24], (D_MODEL,)),
        'ffn2_w_gate': w(k[25], (D_MODEL, D_FF), D_MODEL),
        'ffn2_w_up': w(k[26], (D_MODEL, D_FF), D_MODEL),
        'ffn2_w_down': w(k[27], (D_FF, D_MODEL), D_FF),
        'final_norm': gain(k[28], (D_MODEL,)),
    }


def reference(x, mem, positions, ffn1_norm, ffn1_w_gate, ffn1_w_up, ffn1_w_down, mix_norm, w_in,
              gla_w_a, gla_b_a, gla_o_norm, nsa_q_norm, nsa_k_norm, nsa_cmp_pos_k, nsa_cmp_w1_k,
              nsa_cmp_w2_k, nsa_cmp_pos_v, nsa_cmp_w1_v, nsa_cmp_w2_v, mem_in_norm, w_mem_kv,
              mem_q_norm, mem_k_norm, w_out, ffn2_norm, ffn2_w_gate, ffn2_w_up, ffn2_w_down, final_norm):
    for l in range(DEPTH):
        x = hybrid_layer(x, mem, positions, ffn1_norm[l], ffn1_w_gate[l], ffn1_w_up[l], ffn1_w_down[l],
                         mix_norm[l], w_in[l], gla_w_a[l], gla_b_a[l], gla_o_norm[l], nsa_q_norm[l],
                         nsa_k_norm[l], nsa_cmp_pos_k[l], nsa_cmp_w1_k[l], nsa_cmp_w2_k[l], nsa_cmp_pos_v[l],
                         nsa_cmp_w1_v[l], nsa_cmp_w2_v[l], mem_in_norm[l], w_mem_kv[l], mem_q_norm[l],
                         mem_k_norm[l], w_out[l], ffn2_norm[l], ffn2_w_gate[l], ffn2_w_up[l], ffn2_w_down[l],
                         final_norm[l])
    return x
```

```python
import numpy as np
from contextlib import ExitStack
import ml_dtypes
import concourse.bass as bass
import concourse.mybir as mybir
from concourse.bass_utils import run_bass_kernel_spmd

F32 = mybir.dt.float32
BF16 = mybir.dt.bfloat16
I32 = mybir.dt.int32
ALU = mybir.AluOpType
AF = mybir.ActivationFunctionType

NCORES = 8
DBG = {}
S = 8192
D = 2048
DFF = 5632
NTOK = S // NCORES
NT = NTOK // 128
KC = D // 128
FC = DFF // 128
EPS = 1e-6
NEG = -30000.0
SCALE = 128 ** -0.5
PI = float(np.pi)

C_GQ, C_GK, C_GV, C_GR, C_GA = 0, 256, 512, 1024, 1536
C_NQ, C_KC, C_VC, C_KS, C_VS, C_KW, C_VW, C_NG, C_MQ = 1552, 2576, 2832, 3088, 3344, 3600, 3856, 4112, 4136
DIN = 4648


class Buf:
    __slots__ = ("w", "r")

    def __init__(self):
        self.w = {}
        self.r = {}


class Tile:
    def __init__(self, t, name=""):
        self.t = t
        self.name = name
        self.b = Buf()

    def __getitem__(self, k):
        return self.t[k]


class Prog:
    ENGS = ("pe", "act", "dve", "pool", "sp")
    SAME_ENG_SYNC = ("act", "dve", "pool")

    def __init__(self, nc, es, n_dma_sems=8):
        self.nc = nc
        self.ops = {e: [] for e in self.ENGS}
        self.csem = {}
        self.ccnt = {}
        self.semobj = {}
        for e in ("pe", "act", "dve", "pool"):
            s = es.enter_context(nc.semaphore("c_" + e))
            self.csem[e] = s
            self.semobj[id(s)] = s
            self.ccnt[e] = 0
        self.dsem = {}
        self.dcnt = {}
        self.dnext = {}
        for e in ("sp", "act", "pool"):
            self.dsem[e] = [es.enter_context(nc.semaphore("d_%s%d" % (e, i))) for i in range(n_dma_sems)]
            for s in self.dsem[e]:
                self.semobj[id(s)] = s
            self.dcnt[e] = [0] * n_dma_sems
            self.dnext[e] = 0
        self.waited = {e: {} for e in self.ENGS}
        self.n_wait = 0
        self.n_op = 0

    def _need(self, eng, tok):
        teng, key, val = tok
        if teng == eng and key == id(self.csem.get(eng)) and eng not in self.SAME_ENG_SYNC:
            return
        if self.waited[eng].get(key, 0) >= val:
            return
        self.waited[eng][key] = val
        self.ops[eng].append(("wait", self.semobj[key], val))
        self.n_wait += 1

    def _deps(self, eng, rd, wr):
        for t in rd:
            for tok in t.b.w.values():
                self._need(eng, tok)
        for t in wr:
            for tok in t.b.w.values():
                self._need(eng, tok)
            for tok in t.b.r.values():
                self._need(eng, tok)

    def _mark(self, tok, rd, wr):
        key = tok[1]
        for t in rd:
            t.b.r[key] = tok
        for t in wr:
            t.b.w[key] = tok

    def op(self, eng, fn, rd=(), wr=()):
        self._deps(eng, rd, wr)
        self.ccnt[eng] += 1
        sem = self.csem[eng]
        tok = (eng, id(sem), self.ccnt[eng])
        self.ops[eng].append(("op", fn, sem, 1))
        self._mark(tok, rd, wr)
        self.n_op += 1
        return tok

    def dma_fn(self, eng, fn, rd=(), wr=()):
        k = self.dnext[eng]
        self.dnext[eng] = (k + 1) % len(self.dsem[eng])
        sem = self.dsem[eng][k]
        if self.dcnt[eng][k] > 0:
            self._need(eng, (eng + "_dma", id(sem), 16 * self.dcnt[eng][k]))
        self._deps(eng, rd, wr)
        self.dcnt[eng][k] += 1
        tok = (eng + "_dma", id(sem), 16 * self.dcnt[eng][k])
        self.ops[eng].append(("op", fn, sem, 16))
        self._mark(tok, rd, wr)
        self.n_op += 1
        return tok

    def dma(self, eng, out_ap, in_ap, rd=(), wr=()):
        return self.dma_fn(eng, lambda e: e.dma_start(out=out_ap, in_=in_ap), rd, wr)

    def barrier(self):
        for eng in self.ENGS:
            for e2 in ("pe", "act", "dve", "pool"):
                if self.ccnt[e2] > 0:
                    self._need(eng, (e2 + "_x", id(self.csem[e2]), self.ccnt[e2]))
            for q in ("sp", "act", "pool"):
                for k, sem in enumerate(self.dsem[q]):
                    if self.dcnt[q][k] > 0:
                        self._need(eng, (q + "_dma", id(sem), 16 * self.dcnt[q][k]))

    def emit(self):
        nc = self.nc
        ops = self.ops

        def replay(eng_obj, lst):
            for o in lst:
                if o[0] == "wait":
                    eng_obj.wait_ge(o[1], o[2])
                else:
                    o[1](eng_obj).then_inc(o[2], o[3])

        with nc.Block() as block:
            @block.sync
            def _(e):
                replay(e, ops["sp"])

            @block.scalar
            def _(e):
                replay(e, ops["act"])

            @block.vector
            def _(e):
                replay(e, ops["dve"])

            @block.gpsimd
            def _(e):
                replay(e, ops["pool"])

            @block.tensor
            def _(e):
                replay(e, ops["pe"])


class K:
    def __init__(self, nc, es):
        self.nc = nc
        self.P = Prog(nc, es)
        self.uid = 0

    def sb(self, es, name, shape, dt):
        self.uid += 1
        return Tile(es.enter_context(self.nc.sbuf_tensor("%s_%d" % (name, self.uid), shape, dt)), name)

    def ps(self, es, name, shape, dt):
        self.uid += 1
        return Tile(es.enter_context(self.nc.psum_tensor("%s_%d" % (name, self.uid), shape, dt)), name)

    def dram_in(self, name, shape, dt):
        return Tile(self.nc.dram_tensor(name, list(shape), dt, kind="ExternalInput").ap(), name)

    def dram_out(self, name, shape, dt):
        return Tile(self.nc.dram_tensor(name, list(shape), dt, kind="ExternalOutput").ap(), name)

    def dram_tmp(self, name, shape, dt):
        return Tile(self.nc.dram_tensor(name, list(shape), dt).ap(), name)


def make_consts(k, es):
    P = k.P
    c = {}
    c["ident"] = k.sb(es, "ident", [128, 128], BF16)
    c["identf"] = k.sb(es, "identf", [128, 128], F32)
    c["triu"] = k.sb(es, "triu", [128, 128], F32)
    c["ones"] = k.sb(es, "ones", [128, 128], F32)
    for nm in ("ident", "identf"):
        t = c[nm]
        P.op("pool", lambda e, t=t: e.memset(t[:], 1.0), wr=[t])
        P.op("pool", lambda e, t=t: e.affine_select(out=t[:], in_=t[:], pattern=[[-1, 128]], compare_op=ALU.is_equal,
                                                    fill=0.0, base=0, channel_multiplier=1), rd=[t], wr=[t])
    t = c["triu"]
    P.op("pool", lambda e, t=t: e.memset(t[:], 1.0), wr=[t])
    P.op("pool", lambda e, t=t: e.affine_select(out=t[:], in_=t[:], pattern=[[1, 128]], compare_op=ALU.is_ge,
                                                fill=0.0, base=0, channel_multiplier=-1), rd=[t], wr=[t])
    t = c["ones"]
    P.op("pool", lambda e, t=t: e.memset(t[:], 1.0), wr=[t])
    return c


def norm_transpose(k, c, src, gain_dram, dstT, tag, ntiles=NT):
    P = k.P
    with ExitStack() as es:
        gb = k.sb(es, "gb" + tag, [128, D], F32)
        P.dma("sp", gb[:], gain_dram[:].partition_broadcast(128), wr=[gb])
        xs = [k.sb(es, "nx%d" % i, [128, D], F32) for i in range(2)]
        xn = [k.sb(es, "nxn%d" % i, [128, D], BF16) for i in range(2)]
        junk = k.sb(es, "njunk", [128, D], BF16)
        st = k.sb(es, "nst", [128, 3 * NT], F32)
        pts = [k.ps(es, "npt%d" % i, [128, 8, 128], BF16) for i in range(2)]
        P.op("dve", lambda e: e.memset(st[:], 0.0), wr=[st])
        for tt in range(ntiles):
            x = xs[tt % 2]
            n = xn[tt % 2]
            P.dma("sp", x[:], src[tt * 128:(tt + 1) * 128, :], rd=[src], wr=[x])
            P.op("act", lambda e, x=x, tt=tt: e.activation(out=junk[:], in_=x[:], func=AF.Square, accum_out=st[:, tt:tt + 1]),
                 rd=[x], wr=[junk, st])
            P.op("act", lambda e, tt=tt: e.activation(out=st[:, NT + tt:NT + tt + 1], in_=st[:, tt:tt + 1], func=AF.Sqrt,
                                                      scale=1.0 / D, bias=EPS), rd=[st], wr=[st])
            P.op("dve", lambda e, tt=tt: e.reciprocal(out=st[:, 2 * NT + tt:2 * NT + tt + 1], in_=st[:, NT + tt:NT + tt + 1]),
                 rd=[st], wr=[st])
            P.op("dve", lambda e, x=x, n=n, tt=tt: e.scalar_tensor_tensor(out=n[:], in0=x[:], scalar=st[:, 2 * NT + tt:2 * NT + tt + 1],
                                                                         in1=gb[:], op0=ALU.mult, op1=ALU.mult),
                 rd=[x, st, gb], wr=[n])
            for half in range(2):
                pt = pts[half]
                for i in range(8):
                    kc = half * 8 + i
                    P.op("pe", lambda e, pt=pt, n=n, i=i, kc=kc: e.transpose(out=pt[:, i, :], in_=n[:, kc * 128:(kc + 1) * 128],
                                                                            identity=c["ident"][:]),
                         rd=[n, c["ident"]], wr=[pt])
                eng = "act" if half == 0 else "dve"
                if eng == "act":
                    P.op("act", lambda e, pt=pt, half=half, tt=tt: e.copy(out=dstT[:, half * 8:half * 8 + 8, tt * 128:(tt + 1) * 128], in_=pt[:]),
                         rd=[pt], wr=[dstT])
                else:
                    P.op("dve", lambda e, pt=pt, half=half, tt=tt: e.tensor_copy(out=dstT[:, half * 8:half * 8 + 8, tt * 128:(tt + 1) * 128], in_=pt[:]),
                         rd=[pt], wr=[dstT])
    P.barrier()


def ffn_phase(k, c, xin, xout, gain_dram, wg, wu, wd, tag):
    P = k.P
    with ExitStack() as es:
        hT = k.sb(es, "hT" + tag, [128, FC, NTOK], BF16)
        with ExitStack() as es1:
            xnT = k.sb(es1, "xnT" + tag, [128, KC, NTOK], BF16)
            norm_transpose(k, c, xin, gain_dram, xnT, tag)
            with ExitStack() as es2:
                NS = 6
                ring = [k.sb(es2, "wr%d" % i, [128, KC, 128], BF16) for i in range(NS)]
                sg = [k.sb(es2, "sg%d" % i, [128, 512], F32) for i in range(2)]
                pg = [k.ps(es2, "pg%d" % i, [128, 512], F32) for i in range(2)]
                pu = [k.ps(es2, "pu%d" % i, [128, 512], F32) for i in range(2)]

                def load(fc):
                    for m, w in enumerate((wg, wu)):
                        slot = ring[(2 * fc + m) % NS]
                        P.dma("pool", slot[:], w[:, fc * 128:(fc + 1) * 128].rearrange("(k p) f -> p k f", p=128),
                              rd=[w], wr=[slot])

                for fc in range(min(3, FC)):
                    load(fc)
                it = 0
                for fc in range(FC):
                    sgate = ring[(2 * fc) % NS]
                    sup = ring[(2 * fc + 1) % NS]
                    for th in range(2):
                        g = pg[it % 2]
                        u = pu[it % 2]
                        s = sg[it % 2]
                        it += 1
                        for kk in range(KC):
                            P.op("pe", lambda e, g=g, sgate=sgate, kk=kk, th=th: e.matmul(
                                g[:], lhsT=sgate[:, kk, :], rhs=xnT[:, kk, th * 512:(th + 1) * 512], start=(kk == 0), stop=(kk == KC - 1)),
                                rd=[sgate, xnT], wr=[g])
                        for kk in range(KC):
                            P.op("pe", lambda e, u=u, sup=sup, kk=kk, th=th: e.matmul(
                                u[:], lhsT=sup[:, kk, :], rhs=xnT[:, kk, th * 512:(th + 1) * 512], start=(kk == 0), stop=(kk == KC - 1)),
                                rd=[sup, xnT], wr=[u])
                        P.op("act", lambda e, g=g, s=s: e.activation(out=s[:], in_=g[:], func=AF.Silu), rd=[g], wr=[s])
                        P.op("dve", lambda e, u=u, s=s, fc=fc, th=th: e.tensor_tensor(
                            out=hT[:, fc, th * 512:(th + 1) * 512], in0=s[:], in1=u[:], op=ALU.mult), rd=[s, u], wr=[hT])
                    if fc + 3 < FC:
                        load(fc + 3)
            P.barrier()
        with ExitStack() as es3:
            NDS = D // 256
            wdb = [k.sb(es3, "wd%d" % i, [128, FC, 256], BF16) for i in range(2)]
            xsl = [k.sb(es3, "xsl%d" % i, [128, 256], F32) for i in range(3)]
            ysl = [k.sb(es3, "ysl%d" % i, [128, 256], F32) for i in range(3)]
            pd = [k.ps(es3, "pd%d" % i, [128, 256], F32) for i in range(2)]

            def loadd(ds):
                buf = wdb[ds % 2]
                for q4 in range(4):
                    P.dma("pool", buf[:, q4 * 11:(q4 + 1) * 11, :],
                          wd[q4 * 11 * 128:(q4 + 1) * 11 * 128, ds * 256:(ds + 1) * 256].rearrange("(c p) n -> p c n", p=128),
                          rd=[wd], wr=[buf])

            loadd(0)
            it = 0
            for ds in range(NDS):
                if ds + 1 < NDS:
                    loadd(ds + 1)
                buf = wdb[ds % 2]
                for tt in range(NT):
                    p = pd[it % 2]
                    xs_ = xsl[it % 3]
                    ys_ = ysl[it % 3]
                    it += 1
                    P.dma("sp", xs_[:], xin[tt * 128:(tt + 1) * 128, ds * 256:(ds + 1) * 256], rd=[xin], wr=[xs_])
                    for fc in range(FC):
                        P.op("pe", lambda e, p=p, buf=buf, fc=fc, tt=tt: e.matmul(
                            p[:], lhsT=hT[:, fc, tt * 128:(tt + 1) * 128], rhs=buf[:, fc, :], start=(fc == 0), stop=(fc == FC - 1)),
                            rd=[hT, buf], wr=[p])
                    P.op("dve", lambda e, p=p, xs_=xs_, ys_=ys_: e.scalar_tensor_tensor(
                        out=ys_[:], in0=p[:], scalar=0.5, in1=xs_[:], op0=ALU.mult, op1=ALU.add), rd=[p, xs_], wr=[ys_])
                    P.dma("act", xout[tt * 128:(tt + 1) * 128, ds * 256:(ds + 1) * 256], ys_[:], rd=[ys_], wr=[xout])
        P.barrier()


def bc_heads(ap2d, H):
    return ap2d.unsqueeze(1).to_broadcast([128, H, ap2d.shape[-1]])


def rope_tables(k, c, es, pos_dram, inv_dram, n_tiles):
    P = k.P
    cos = k.sb(es, "cos", [128, n_tiles, 64], F32)
    sin = k.sb(es, "sin", [128, n_tiles, 64], F32)
    with ExitStack() as e2:
        pi_ = k.sb(e2, "posi", [128, n_tiles], I32)
        pf = k.sb(e2, "posf", [128, n_tiles], F32)
        inv = k.sb(e2, "inv", [128, 64], F32)
        ang = k.sb(e2, "ang", [128, 64], F32)
        a2 = k.sb(e2, "ang2", [128, 64], F32)
        kf = k.sb(e2, "kf", [128, 64], F32)
        ki = k.sb(e2, "ki", [128, 64], I32)
        m1 = k.sb(e2, "m1", [128, 64], F32)
        P.dma("sp", pi_[:], pos_dram[:], wr=[pi_])
        P.dma("sp", inv[:], inv_dram[:].partition_broadcast(128), wr=[inv])
        P.op("dve", lambda e: e.tensor_copy(out=pf[:], in_=pi_[:]), rd=[pi_], wr=[pf])
        for tt in range(n_tiles):
            P.op("dve", lambda e, tt=tt: e.tensor_scalar(out=ang[:], in0=inv[:], scalar1=pf[:, tt:tt + 1], scalar2=None, op0=ALU.mult),
                 rd=[inv, pf], wr=[ang])
            for which, dst in ((0, sin), (1, cos)):
                src = ang
                if which == 1:
                    P.op("dve", lambda e: e.tensor_scalar(out=a2[:], in0=ang[:], scalar1=PI / 2, scalar2=None, op0=ALU.add), rd=[ang], wr=[a2])
                    src = a2
                P.op("dve", lambda e, src=src: e.tensor_scalar(out=kf[:], in0=src[:], scalar1=1.0 / (2 * PI), scalar2=None, op0=ALU.mult), rd=[src], wr=[kf])
                P.op("dve", lambda e: e.tensor_copy(out=ki[:], in_=kf[:]), rd=[kf], wr=[ki])
                P.op("dve", lambda e: e.tensor_copy(out=kf[:], in_=ki[:]), rd=[ki], wr=[kf])
                P.op("dve", lambda e, src=src: e.scalar_tensor_tensor(out=a2[:], in0=kf[:], scalar=-2 * PI, in1=src[:], op0=ALU.mult, op1=ALU.add),
                     rd=[kf, src], wr=[a2])
                P.op("dve", lambda e: e.tensor_scalar(out=m1[:], in0=a2[:], scalar1=PI, scalar2=-2 * PI, op0=ALU.is_gt, op1=ALU.mult), rd=[a2], wr=[m1])
                P.op("dve", lambda e: e.tensor_tensor(out=a2[:], in0=a2[:], in1=m1[:], op=ALU.add), rd=[a2, m1], wr=[a2])
                P.op("dve", lambda e: e.tensor_scalar(out=m1[:], in0=a2[:], scalar1=-PI, scalar2=2 * PI, op0=ALU.is_lt, op1=ALU.mult), rd=[a2], wr=[m1])
                P.op("dve", lambda e: e.tensor_tensor(out=a2[:], in0=a2[:], in1=m1[:], op=ALU.add), rd=[a2, m1], wr=[a2])
                P.op("act", lambda e, dst=dst, tt=tt: e.activation(out=dst[:, tt, :], in_=a2[:], func=AF.Sin), rd=[a2], wr=[dst])
        P.barrier()
    return cos, sin


class HeadNorm:
    def __init__(self, k, es, tag):
        self.k = k
        self.junk = k.sb(es, "hjunk" + tag, [128, 128], BF16)
        self.st = k.sb(es, "hst" + tag, [128, 24], F32)
        self.xn = k.sb(es, "hxn" + tag, [128, 4, 128], F32)
        self.t1 = k.sb(es, "ht1" + tag, [128, 4, 64], F32)
        self.t2 = k.sb(es, "ht2" + tag, [128, 4, 64], F32)
        self.t3 = k.sb(es, "ht3" + tag, [128, 4, 64], F32)
        self.t4 = k.sb(es, "ht4" + tag, [128, 4, 64], F32)

    def run(self, src, srcT, H, gain, dst, dstT, cos=None, sin=None):
        P = self.k.P
        st, junk, xn = self.st, self.junk, self.xn
        P.op("dve", lambda e: e.memset(st[:, 0:H], 0.0), wr=[st])
        for h in range(H):
            P.op("act", lambda e, h=h: e.activation(out=junk[:], in_=src[:, h, :], func=AF.Square, accum_out=st[:, h:h + 1]),
                 rd=[srcT], wr=[junk, st])
        P.op("act", lambda e: e.activation(out=st[:, 8:8 + H], in_=st[:, 0:H], func=AF.Sqrt, scale=1.0 / 128, bias=EPS), rd=[st], wr=[st])
        P.op("dve", lambda e: e.reciprocal(out=st[:, 16:16 + H], in_=st[:, 8:8 + H]), rd=[st], wr=[st])
        rope = cos is not None
        for h in range(H):
            o = xn[:, h, :] if rope else dst[:, h, :]
            P.op("dve", lambda e, h=h, o=o: e.scalar_tensor_tensor(out=o, in0=src[:, h, :], scalar=st[:, 16 + h:17 + h], in1=gain[:],
                                                                  op0=ALU.mult, op1=ALU.mult),
                 rd=[srcT, st, gain], wr=[xn if rope else dstT])
        if rope:
            t1, t2, t3, t4 = self.t1, self.t2, self.t3, self.t4
            x1 = xn[:, 0:H, 0:64]
            x2 = xn[:, 0:H, 64:128]
            cb = bc_heads(cos, H)
            sb_ = bc_heads(sin, H)
            P.op("dve", lambda e: e.tensor_tensor(out=t1[:, 0:H, :], in0=x1, in1=cb, op=ALU.mult), rd=[xn], wr=[t1])
            P.op("dve", lambda e: e.tensor_tensor(out=t2[:, 0:H, :], in0=x2, in1=sb_, op=ALU.mult), rd=[xn], wr=[t2])
            P.op("dve", lambda e: e.tensor_tensor(out=dst[:, :, 0:64], in0=t1[:, 0:H, :], in1=t2[:, 0:H, :], op=ALU.subtract), rd=[t1, t2], wr=[dstT])
            P.op("pool", lambda e: e.tensor_tensor(out=t3[:, 0:H, :], in0=x2, in1=cb, op=ALU.mult), rd=[xn], wr=[t3])
            P.op("pool", lambda e: e.tensor_tensor(out=t4[:, 0:H, :], in0=x1, in1=sb_, op=ALU.mult), rd=[xn], wr=[t4])
            P.op("pool", lambda e: e.tensor_tensor(out=dst[:, :, 64:128], in0=t3[:, 0:H, :], in1=t4[:, 0:H, :], op=ALU.add), rd=[t3, t4], wr=[dstT])


def transposes(k, c, src, srcT, n, pt, dst_fn, dstT, eng="act"):
    P = k.P
    for i in range(n):
        P.op("pe", lambda e, i=i: e.transpose(out=pt[:, i, :], in_=src[:, i, :], identity=c["ident"][:]), rd=[srcT, c["ident"]], wr=[pt])
    for i in range(n):
        d = dst_fn(i)
        if eng == "act":
            P.op("act", lambda e, i=i, d=d: e.copy(out=d, in_=pt[:, i, :]), rd=[pt], wr=[dstT])
        else:
            P.op("dve", lambda e, i=i, d=d: e.tensor_copy(out=d, in_=pt[:, i, :]), rd=[pt], wr=[dstT])


def inproj_phase(k, c, x1, W, O, es_out):
    P = k.P
    mqT = k.sb(es_out, "mqT", [128, 4, NTOK], BF16)
    with ExitStack() as es:
        hT = k.sb(es, "hTmix", [128, KC, NTOK], BF16)
        norm_transpose(k, c, x1, W["mix_norm"], hT, "m")
        cos, sin = rope_tables(k, c, es, W["pos"], W["rope_inv"], NT)
        gains = {}
        for nm, src in (("q", W["nsa_q_norm"][0:1, :]), ("k1", W["nsa_k_norm"][1:2, :]), ("k2", W["nsa_k_norm"][2:3, :]), ("mq", W["mem_q_norm"][0:1, :])):
            g = k.sb(es, "gain" + nm, [128, 128], F32)
            P.dma("sp", g[:], src.partition_broadcast(128), wr=[g])
            gains[nm] = g
        wbuf = [k.sb(es, "wib%d" % i, [128, KC, 512], BF16) for i in range(2)]
        pp = [k.ps(es, "pp%d" % i, [128, 512], F32) for i in range(2)]
        ptr = [k.ps(es, "ptr%d" % i, [128, 8, 128], BF16) for i in range(2)]
        pz = k.ps(es, "pz", [128, 512], F32)
        pB = k.ps(es, "pB", [128, 256], F32)
        pBL = k.ps(es, "pBL", [128, 256], F32)
        pm = k.ps(es, "pm", [128, 512], F32)
        hn = HeadNorm(k, es, "a")
        qT = k.sb(es, "qT", [128, 8, NTOK], BF16)
        kT = k.sb(es, "kT", [128, 4, NTOK], BF16)
        cT = k.sb(es, "cT", [128, 4, NTOK], BF16)
        vtm = k.sb(es, "vtm", [128, 4, NT, 128], BF16)
        gates = k.sb(es, "gates", [128, NT, 24], F32)
        gqT = k.sb(es, "gqT", [128, 2, NTOK], BF16)
        gkT = k.sb(es, "gkT", [128, 2, NTOK], BF16)
        gv = k.sb(es, "gv", [128, NT, 512], BF16)
        sgr = k.sb(es, "sgr", [128, NT, 512], F32)
        gD = k.sb(es, "gD", [128, NT, 2], F32)
        ga1T = k.sb(es, "ga1T", [17, NTOK], BF16)
        wab = k.sb(es, "wab", [17, 256], BF16)
        hb = k.sb(es, "hb", [128, 4, 128], BF16)
        nload = [0]

        def load_w(c0, n):
            b = wbuf[nload[0] % 2]
            nload[0] += 1
            P.dma("pool", b[:, :, 0:n], W["w_in"][:, c0:c0 + n].rearrange("(k p) f -> p k f", p=128), rd=[W["w_in"]], wr=[b])
            return b

        nproj = [0]

        def proj_tm(b, n, tt):
            p = pp[nproj[0] % 2]
            nproj[0] += 1
            for kk in range(KC):
                P.op("pe", lambda e, p=p, kk=kk: e.matmul(p[:, 0:n], lhsT=hT[:, kk, tt * 128:(tt + 1) * 128], rhs=b[:, kk, 0:n],
                                                          start=(kk == 0), stop=(kk == KC - 1)), rd=[hT, b], wr=[p])
            return p

        ntr = [0]

        def next_ptr():
            p = ptr[ntr[0] % 2]
            ntr[0] += 1
            return p

        b = load_w(C_GA, 16)
        P.op("dve", lambda e: e.memset(ga1T[:], 1.0), wr=[ga1T])
        P.dma("pool", wab[0:16, :], W["gla_w_a"][:], wr=[wab])
        P.dma("pool", wab[16:17, :], W["gla_b_a"][:], wr=[wab])
        for th in range(2):
            p = pp[nproj[0] % 2]
            nproj[0] += 1
            for kk in range(KC):
                P.op("pe", lambda e, p=p, kk=kk, th=th, b=b: e.matmul(p[0:16, :], lhsT=b[:, kk, 0:16], rhs=hT[:, kk, th * 512:(th + 1) * 512],
                                                                 start=(kk == 0), stop=(kk == KC - 1)), rd=[hT, b], wr=[p])
            P.op("act", lambda e, p=p, th=th: e.copy(out=ga1T[0:16, th * 512:(th + 1) * 512], in_=p[0:16, :]), rd=[p], wr=[ga1T])
        b = load_w(C_KC, 512)
        for i4 in range(4):
            for th in range(2):
                p = pp[nproj[0] % 2]
                nproj[0] += 1
                for kk in range(KC):
                    P.op("pe", lambda e, p=p, kk=kk, th=th, i4=i4, b=b: e.matmul(p[:], lhsT=b[:, kk, i4 * 128:(i4 + 1) * 128],
                                                                            rhs=hT[:, kk, th * 512:(th + 1) * 512],
                                                                            start=(kk == 0), stop=(kk == KC - 1)), rd=[hT, b], wr=[p])
                P.op("act", lambda e, p=p, th=th, i4=i4: e.copy(out=cT[:, i4, th * 512:(th + 1) * 512], in_=p[:]), rd=[p], wr=[cT])
        b = load_w(C_GV, 512)
        for tt in range(NT):
            p = proj_tm(b, 512, tt)
            P.op("act", lambda e, p=p, tt=tt: e.copy(out=gv[:, tt, :], in_=p[:]), rd=[p], wr=[gv])
        b = load_w(C_GR, 512)
        for tt in range(NT):
            p = proj_tm(b, 512, tt)
            P.op("act", lambda e, p=p, tt=tt: e.activation(out=sgr[:, tt, :], in_=p[:], func=AF.Silu), rd=[p], wr=[sgr])
        b = load_w(C_GQ, 512)
        with ExitStack() as e2:
            e1 = k.sb(e2, "ge1", [128, 256], F32)
            la = k.sb(e2, "gla", [128, 256], F32)
            eB = k.sb(e2, "geB", [128, 256], F32)
            enB = k.sb(e2, "genB", [128, 256], F32)
            BLs = k.sb(e2, "gBLs", [128, 256], F32)
            eBLB = k.sb(e2, "geBLB", [128, 256], F32)
            qkt = k.sb(e2, "gqkt", [128, 4, 128], BF16)
            kt2 = k.sb(e2, "gkt2", [128, 256], BF16)
            Ls = [k.sb(e2, "gLs%d" % i, [128, 128], F32) for i in range(2)]
            for tt in range(NT):
                p = proj_tm(b, 512, tt)
                P.op("pe", lambda e, tt=tt: e.matmul(pz[:, 0:256], lhsT=ga1T[:, tt * 128:(tt + 1) * 128], rhs=wab[:], start=True, stop=True),
                     rd=[ga1T, wab], wr=[pz])
                P.op("act", lambda e: e.activation(out=e1[:], in_=pz[:, 0:256], func=AF.Exp, scale=-1.0), rd=[pz], wr=[e1])
                P.op("act", lambda e: e.activation(out=e1[:], in_=e1[:], func=AF.Ln, bias=1.0), rd=[e1], wr=[e1])
                P.op("dve", lambda e: e.tensor_scalar(out=la[:], in0=e1[:], scalar1=-1.0 / 16, scalar2=None, op0=ALU.mult), rd=[e1], wr=[la])
                P.op("pe", lambda e: e.matmul(pB[:], lhsT=c["triu"][:], rhs=la[:], start=True, stop=True), rd=[c["triu"], la], wr=[pB])
                P.op("pe", lambda e: e.matmul(pBL[:], lhsT=c["ones"][:], rhs=la[:], start=True, stop=True), rd=[c["ones"], la], wr=[pBL])
                for t in range(2):
                    P.op("pe", lambda e, t=t: e.matmul(pm[:, t:t + 1], lhsT=la[:, t * 128:(t + 1) * 128], rhs=c["ones"][:, 0:1], start=True, stop=True),
                         rd=[la, c["ones"]], wr=[pm])
                P.op("act", lambda e: e.activation(out=eB[:], in_=pB[:], func=AF.Exp), rd=[pB], wr=[eB])
                P.op("act", lambda e: e.activation(out=enB[:], in_=pB[:], func=AF.Exp, scale=-1.0), rd=[pB], wr=[enB])
                P.op("act", lambda e: e.copy(out=BLs[:], in_=pBL[:]), rd=[pBL], wr=[BLs])
                P.op("dve", lambda e: e.tensor_tensor(out=BLs[:], in0=BLs[:], in1=pB[:], op=ALU.subtract), rd=[BLs, pB], wr=[BLs])
                P.op("act", lambda e: e.activation(out=eBLB[:], in_=BLs[:], func=AF.Exp), rd=[BLs], wr=[eBLB])
                P.op("act", lambda e, tt=tt: e.activation(out=gD[:, tt, :], in_=pm[:, 0:2], func=AF.Exp), rd=[pm], wr=[gD])
                P.op("dve", lambda e, p=p: e.scalar_tensor_tensor(out=qkt[:, 0:2, :], in0=p[:, 0:256].rearrange("p (t n) -> p t n", t=2), scalar=0.125,
                                                                  in1=eB[:].rearrange("p (t n) -> p t n", t=2), op0=ALU.mult, op1=ALU.mult),
                     rd=[p, eB], wr=[qkt])
                P.op("dve", lambda e, p=p: e.tensor_tensor(out=qkt[:, 2:4, :], in0=p[:, 256:512].rearrange("p (t n) -> p t n", t=2),
                                                           in1=enB[:].rearrange("p (t n) -> p t n", t=2), op=ALU.mult), rd=[p, enB], wr=[qkt])
                P.op("dve", lambda e, p=p: e.tensor_tensor(out=kt2[:], in0=p[:, 256:512], in1=eBLB[:], op=ALU.mult), rd=[p, eBLB], wr=[kt2])
                pt = next_ptr()
                transposes(k, c, qkt, qkt, 4, pt,
                           lambda i, tt=tt: (gqT if i < 2 else gkT)[:, i % 2, tt * 128:(tt + 1) * 128], gqT, eng="act")
                gkT.b.w.update(gqT.b.w)
                for t in range(2):
                    P.op("pe", lambda e, t=t, tt=tt: e.matmul(pz[:, 256:512], lhsT=kt2[:, t * 128:(t + 1) * 128], rhs=gv[:, tt, t * 256:(t + 1) * 256],
                                                             start=True, stop=True), rd=[kt2, gv], wr=[pz])
                    L = Ls[t]
                    P.op("act", lambda e, L=L: e.copy(out=L[0:64, :], in_=pz[0:64, 256:384]), rd=[pz], wr=[L])
                    P.op("dve", lambda e, L=L: e.tensor_copy(out=L[64:128, :], in_=pz[64:128, 384:512]), rd=[pz], wr=[L])
                    P.dma("sp", O["gL"][tt, t], L[:], rd=[L], wr=[O["gL"]])
            P.barrier()
        for g in range(2):
            b = load_w(C_NQ + 512 * g, 512)
            for tt in range(NT):
                p = proj_tm(b, 512, tt)
                hn.run(p[:].rearrange("p (h n) -> p h n", h=4), p, 4, gains["q"], hb[:], hb, cos[:, tt, :], sin[:, tt, :])
                pt = next_ptr()
                transposes(k, c, hb, hb, 4, pt, lambda i, tt=tt, g=g: qT[:, 4 * g + i, tt * 128:(tt + 1) * 128], qT, eng="act")
        for kind, (c0, gname) in enumerate(((C_KS, "k1"), (C_KW, "k2"))):
            b = load_w(c0, 512)
            for tt in range(NT):
                p = proj_tm(b, 512, tt)
                hn.run(p[:, 0:256].rearrange("p (h n) -> p h n", h=2), p, 2, gains[gname], hb[:, 0:2, :], hb, cos[:, tt, :], sin[:, tt, :])
                P.op("act", lambda e, p=p, tt=tt, kind=kind: e.copy(out=vtm[:, 2 * kind:2 * kind + 2, tt, :],
                                                                    in_=p[:, 256:512].rearrange("p (h n) -> p h n", h=2)), rd=[p], wr=[vtm])
                pt = next_ptr()
                transposes(k, c, hb, hb, 2, pt, lambda i, tt=tt, kind=kind: kT[:, 2 * kind + i, tt * 128:(tt + 1) * 128], kT, eng="dve")
        b = load_w(C_MQ, 512)
        for tt in range(NT):
            p = proj_tm(b, 512, tt)
            hn.run(p[:].rearrange("p (h n) -> p h n", h=4), p, 4, gains["mq"], hb[:], hb)
            pt = next_ptr()
            transposes(k, c, hb, hb, 4, pt, lambda i, tt=tt: mqT[:, i, tt * 128:(tt + 1) * 128], mqT, eng="act")
        b = load_w(C_NG, 24)
        for tt in range(NT):
            p = proj_tm(b, 24, tt)
            P.op("act", lambda e, p=p, tt=tt: e.activation(out=gates[:, tt, :], in_=p[:, 0:24], func=AF.Sigmoid), rd=[p], wr=[gates])
        P.dma("sp", O["qT"][:], qT[:], rd=[qT], wr=[O["qT"]])
        P.dma("sp", O["kT"][:], kT[:], rd=[kT], wr=[O["kT"]])
        P.dma("sp", O["cT"][:], cT[:], rd=[cT], wr=[O["cT"]])
        P.dma("sp", O["vtm"][:], vtm[:], rd=[vtm], wr=[O["vtm"]])
        P.dma("sp", O["gates"][:], gates[:], rd=[gates], wr=[O["gates"]])
        P.dma("sp", O["gqT"][:], gqT[:], rd=[gqT], wr=[O["gqT"]])
        P.dma("sp", O["gkT"][:], gkT[:], rd=[gkT], wr=[O["gkT"]])
        P.dma("sp", O["gv"][:], gv[:], rd=[gv], wr=[O["gv"]])
        P.dma("sp", O["sgr"][:], sgr[:], rd=[sgr], wr=[O["sgr"]])
        P.dma("sp", O["gD"][:], gD[:], rd=[gD], wr=[O["gD"]])
        P.barrier()
    return mqT


def mem_phase(k, c, W, mqT, O):
    P = k.P
    with ExitStack() as es:
        memT = k.sb(es, "memT", [128, KC, 256], BF16)
        norm_transpose(k, c, W["mem"], W["mem_in_norm"], memT, "mem", ntiles=2)
        gk = k.sb(es, "gainmk", [128, 128], F32)
        P.dma("sp", gk[:], W["mem_k_norm"][0:1, :].partition_broadcast(128), wr=[gk])
        wk = k.sb(es, "wmk", [128, KC, 512], BF16)
        wv = k.sb(es, "wmv", [128, KC, 512], BF16)
        P.dma("pool", wk[:], W["w_mem_kv"][:, 0:512].rearrange("(k p) f -> p k f", p=128), wr=[wk])
        P.dma("pool", wv[:], W["w_mem_kv"][:, 512:1024].rearrange("(k p) f -> p k f", p=128), wr=[wv])
        mkT = k.sb(es, "mkT", [128, 4, 256], BF16)
        mv = k.sb(es, "mv", [128, 2, 4, 132], BF16)
        hb = k.sb(es, "mhb", [128, 4, 128], BF16)
        omem = k.sb(es, "omem", [128, NT, 512], F32)
        pT = [k.sb(es, "mpT%d" % i, [128, 512], BF16) for i in range(4)]
        rs = k.sb(es, "mrs", [128, 2], F32)
        hn = HeadNorm(k, es, "m")
        pk = [k.ps(es, "mpk%d" % i, [128, 512], F32) for i in range(2)]
        ptr = k.ps(es, "mptr", [128, 8, 128], BF16)
        pS = [k.ps(es, "mpS%d" % i, [128, 512], F32) for i in range(2)]
        pO = [k.ps(es, "mpO%d" % i, [128, 132], F32) for i in range(2)]
        P.op("pool", lambda e: e.memset(mv[:], 1.0), wr=[mv])
        for mt in range(2):
            p = pk[0]
            for kk in range(KC):
                P.op("pe", lambda e, p=p, kk=kk, mt=mt: e.matmul(p[:], lhsT=memT[:, kk, mt * 128:(mt + 1) * 128], rhs=wk[:, kk, :],
                                                                 start=(kk == 0), stop=(kk == KC - 1)), rd=[memT, wk], wr=[p])
            hn.run(p[:].rearrange("p (h n) -> p h n", h=4), p, 4, gk, hb[:], hb)
            transposes(k, c, hb, hb, 4, ptr, lambda i, mt=mt: mkT[:, i, mt * 128:(mt + 1) * 128], mkT, eng="act")
            p = pk[1]
            for kk in range(KC):
                P.op("pe", lambda e, p=p, kk=kk, mt=mt: e.matmul(p[:], lhsT=memT[:, kk, mt * 128:(mt + 1) * 128], rhs=wv[:, kk, :],
                                                                 start=(kk == 0), stop=(kk == KC - 1)), rd=[memT, wv], wr=[p])
            P.op("act", lambda e, p=p, mt=mt: e.copy(out=mv[:, mt, :, 0:128], in_=p[:].rearrange("p (h n) -> p h n", h=4)), rd=[p], wr=[mv])
        it = 0
        io = 0
        for h in range(4):
            for th in range(2):
                pts_ = []
                for mt in range(2):
                    ps_ = pS[it % 2]
                    pt_ = pT[it % 4]
                    it += 1
                    P.op("pe", lambda e, ps_=ps_, h=h, mt=mt, th=th: e.matmul(ps_[:], lhsT=mkT[:, h, mt * 128:(mt + 1) * 128],
                                                                             rhs=mqT[:, h, th * 512:(th + 1) * 512], start=True, stop=True),
                         rd=[mkT, mqT], wr=[ps_])
                    P.op("act", lambda e, ps_=ps_, pt_=pt_: e.activation(out=pt_[:], in_=ps_[:], func=AF.Exp, scale=SCALE), rd=[ps_], wr=[pt_])
                    pts_.append(pt_)
                for tq in range(4):
                    tt = th * 4 + tq
                    po = pO[io % 2]
                    io += 1
                    for mt in range(2):
                        P.op("pe", lambda e, po=po, mt=mt, tq=tq, h=h, pt_=pts_[mt]: e.matmul(
                            po[:, 0:129], lhsT=pt_[:, tq * 128:(tq + 1) * 128], rhs=mv[:, mt, h, 0:129], start=(mt == 0), stop=(mt == 1)),
                            rd=[pts_[mt], mv], wr=[po])
                    r = rs[:, (io % 2):(io % 2) + 1]
                    P.op("dve", lambda e, po=po, r=r: e.reciprocal(out=r, in_=po[:, 128:129]), rd=[po], wr=[rs])
                    P.op("dve", lambda e, po=po, r=r, tt=tt, h=h: e.tensor_scalar(out=omem[:, tt, h * 128:(h + 1) * 128], in0=po[:, 0:128],
                                                                                 scalar1=r, scalar2=None, op0=ALU.mult), rd=[po, rs], wr=[omem])
        P.dma("sp", O["omem"][:], omem[:], rd=[omem], wr=[O["omem"]])
        P.barrier()


A_OUT = {
    "x1": ([NTOK, D], F32), "qT": ([128, 8, NTOK], BF16), "kT": ([128, 4, NTOK], BF16), "cT": ([128, 4, NTOK], BF16),
    "vtm": ([128, 4, NT, 128], BF16), "gates": ([128, NT, 24], F32), "gqT": ([128, 2, NTOK], BF16), "gkT": ([128, 2, NTOK], BF16),
    "gv": ([128, NT, 512], BF16), "sgr": ([128, NT, 512], F32), "gD": ([128, NT, 2], F32), "gL": ([NT, 2, 128, 128], F32),
    "omem": ([128, NT, 512], F32),
}
A_IN = {
    "x": ([NTOK, D], F32), "pos": ([128, NT], I32), "rope_inv": ([1, 64], F32), "mem": ([256, D], F32),
    "ffn1_norm": ([1, D], F32), "ffn1_w_gate": ([D, DFF], F32), "ffn1_w_up": ([D, DFF], F32), "ffn1_w_down": ([DFF, D], F32),
    "mix_norm": ([1, D], F32), "w_in": ([D, DIN], F32), "gla_w_a": ([16, 256], F32), "gla_b_a": ([1, 256], F32),
    "nsa_q_norm": ([1, 128], F32), "nsa_k_norm": ([3, 128], F32), "mem_in_norm": ([1, D], F32), "w_mem_kv": ([D, 1024], F32),
    "mem_q_norm": ([1, 128], F32), "mem_k_norm": ([1, 128], F32),
}


def build_A(skip_ffn=False):
    nc = bass.Bass("TRN2", target_bir_lowering=False)
    with ExitStack() as es:
        k = K(nc, es)
        W = {n: k.dram_in(n, sh, dt) for n, (sh, dt) in A_IN.items()}
        O = {n: k.dram_out(n, sh, dt) for n, (sh, dt) in A_OUT.items()}
        c = make_consts(k, es)
        k.P.barrier()
        if skip_ffn:
            src = W["x"]
        else:
            ffn_phase(k, c, W["x"], O["x1"], W["ffn1_norm"], W["ffn1_w_gate"], W["ffn1_w_up"], W["ffn1_w_down"], "1")
            src = O["x1"]
        with ExitStack() as es2:
            mqT = inproj_phase(k, c, src, W, O, es2)
            mem_phase(k, c, W, mqT, O)
        k.P.emit()
    return nc


from contextlib import contextmanager


@contextmanager
def scope(k):
    with ExitStack() as es:
        yield es
        k.P.barrier()


def gla_phase(k, c, W, mix):
    P = k.P
    with scope(k) as es:
        gqT = k.sb(es, "bgqT", [128, 2, NTOK], BF16)
        gkT = k.sb(es, "bgkT", [128, 2, NTOK], BF16)
        gv = k.sb(es, "bgv", [128, NT, 512], BF16)
        sgr = k.sb(es, "bsgr", [128, NT, 512], F32)
        gD = k.sb(es, "bgD", [128, 64, 2], F32)
        oh = k.sb(es, "boh", [128, 8], F32)
        go = k.sb(es, "bgo", [128, 128], F32)
        P.dma("sp", gqT[:], W["gqT"][:], wr=[gqT])
        P.dma("sp", gkT[:], W["gkT"][:], wr=[gkT])
        P.dma("sp", gv[:], W["gv"][:], wr=[gv])
        P.dma("sp", sgr[:], W["sgr"][:], wr=[sgr])
        P.dma("sp", gD[:], W["gDf"][:], wr=[gD])
        P.dma("sp", oh[:], W["onehot"][:].partition_broadcast(128), wr=[oh])
        P.dma("sp", go[:], W["gla_o_norm"][:].partition_broadcast(128), wr=[go])
        St = [k.sb(es, "bS%d" % t, [128, 128], F32) for t in range(2)]
        Sm = k.sb(es, "bSm", [128, NT, 2, 128], F32)
        Smb = k.sb(es, "bSmb", [128, NT, 2, 128], BF16)
        Lr = [k.sb(es, "bL%d" % i, [128, 2, 128], F32) for i in range(4)]
        for t in range(2):
            P.op("pool", lambda e, t=t: e.memset(St[t][:], 0.0), wr=[St[t]])
        P.op("pool", lambda e: e.memset(Sm[:], 0.0), wr=[Sm])
        for b in range(64):
            L = Lr[b % 4]
            P.dma("sp", L[:], W["gLf"][b].rearrange("t p n -> p t n"), rd=[W["gLf"]], wr=[L])
            j, o = b // 8, b % 8
            for t in range(2):
                P.op("dve", lambda e, t=t, j=j, o=o: e.scalar_tensor_tensor(out=Sm[:, j, t, :], in0=St[t][:], scalar=oh[:, o:o + 1], in1=Sm[:, j, t, :],
                                                                            op0=ALU.mult, op1=ALU.add), rd=[St[t], oh, Sm], wr=[Sm])
                P.op("dve", lambda e, t=t, b=b, L=L: e.scalar_tensor_tensor(out=St[t][:], in0=St[t][:], scalar=gD[:, b, t:t + 1], in1=L[:, t, :],
                                                                            op0=ALU.mult, op1=ALU.add), rd=[St[t], gD, L], wr=[St[t]])
        P.op("pool", lambda e: e.tensor_copy(out=Smb[:], in_=Sm[:]), rd=[Sm], wr=[Smb])
        pid = k.sb(es, "bpid", [128, 1], I32)
        rmask = k.sb(es, "brmask", [128, 2], F32)
        P.op("pool", lambda e: e.iota(pid[:], pattern=[[0, 1]], base=0, channel_multiplier=1), wr=[pid])
        P.op("dve", lambda e: e.tensor_copy(out=rmask[:, 0:1], in_=pid[:]), rd=[pid], wr=[rmask])
        P.op("dve", lambda e: e.tensor_scalar(out=rmask[:, 1:2], in0=rmask[:, 0:1], scalar1=64.0, scalar2=None, op0=ALU.is_ge), rd=[rmask], wr=[rmask])
        P.op("dve", lambda e: e.tensor_scalar(out=rmask[:, 0:1], in0=rmask[:, 0:1], scalar1=64.0, scalar2=None, op0=ALU.is_lt), rd=[rmask], wr=[rmask])
        gqm = k.sb(es, "bgqm", [128, 4, NTOK], BF16)
        for h in range(4):
            P.op("dve", lambda e, h=h: e.tensor_scalar(out=gqm[:, h, :], in0=gqT[:, h // 2, :], scalar1=rmask[:, h % 2:h % 2 + 1], scalar2=None, op0=ALU.mult),
                 rd=[gqT, rmask], wr=[gqm])
        pA = [k.ps(es, "bpA%d" % i, [128, 4, 128], F32) for i in range(2)]
        pOo = [k.ps(es, "bpO%d" % i, [128, 4, 128], F32) for i in range(2)]
        AT = [k.sb(es, "bAT%d" % i, [128, 4, 128], BF16) for i in range(2)]
        on = k.sb(es, "bon", [128, 4, 128], F32)
        hn = HeadNorm(k, es, "g")
        tri4 = c["triu"][:].unsqueeze(1).to_broadcast([128, 4, 128])
        for j in range(NT):
            pa = pA[j % 2]
            po = pOo[j % 2]
            at = AT[j % 2]
            sl = slice(j * 128, (j + 1) * 128)
            for h in range(4):
                t = h // 2
                P.op("pe", lambda e, pa=pa, h=h, t=t, sl=sl: e.matmul(pa[:, h, :], lhsT=gkT[:, t, sl], rhs=gqm[:, h, sl], start=True, stop=True),
                     rd=[gkT, gqm], wr=[pa])
            P.op("dve", lambda e, pa=pa, at=at: e.tensor_tensor(out=at[:], in0=pa[:], in1=tri4, op=ALU.mult), rd=[pa, c["triu"]], wr=[at])
            for h in range(4):
                t = h // 2
                P.op("pe", lambda e, po=po, at=at, h=h, j=j: e.matmul(po[:, h, :], lhsT=at[:, h, :], rhs=gv[:, j, h * 128:(h + 1) * 128],
                                                                     start=True, stop=False), rd=[at, gv], wr=[po])
                P.op("pe", lambda e, po=po, h=h, t=t, sl=sl, j=j: e.matmul(po[:, h, :], lhsT=gqm[:, h, sl], rhs=Smb[:, j, t, :],
                                                                          start=False, stop=True), rd=[gqm, Smb], wr=[po])
            hn.run(po[:], po, 4, go, on[:], on)
            P.op("dve", lambda e, j=j: e.tensor_tensor(out=mix[:, j, 0:512], in0=on[:].rearrange("p h n -> p (h n)"), in1=sgr[:, j, :], op=ALU.mult),
                 rd=[on, sgr], wr=[mix])


def compress_phase(k, c, W, kcmpT, vcmp):
    P = k.P
    with scope(k) as es:
        cos, sin = rope_tables(k, c, es, W["pos_cmp"], W["rope_inv"], 4)
        gk0 = k.sb(es, "cgk0", [128, 128], F32)
        P.dma("sp", gk0[:], W["nsa_k_norm"][0:1, :].partition_broadcast(128), wr=[gk0])
        cfull = [k.sb(es, "cfull%d" % i, [128, S], BF16) for i in range(2)]
        w1b = [k.sb(es, "cw1%d" % i, [128, 32, 256], BF16) for i in range(2)]
        w2b = [k.sb(es, "cw2%d" % i, [128, 2, 128], BF16) for i in range(2)]
        posf = k.sb(es, "cposf", [32, 128], F32)
        posT = k.sb(es, "cposT", [128, 32], BF16)
        c1 = k.sb(es, "cc1", [128, 2], F32)
        hid = k.sb(es, "chid", [128, 2, 512], BF16)
        hb = k.sb(es, "chb", [128, 1, 128], BF16)
        hn = HeadNorm(k, es, "c")
        ph = [k.ps(es, "cph%d" % i, [128, 512], F32) for i in range(2)]
        pc = k.ps(es, "cpc", [128, 64], F32)
        po = [k.ps(es, "cpo%d" % i, [128, 128], F32) for i in range(2)]
        ptr = k.ps(es, "cptr", [128, 8, 128], BF16)
        P.op("dve", lambda e: e.memset(hid[:], 0.0), wr=[hid])
        it = 0
        for kind, (posn, w1n, w2n) in enumerate((("nsa_cmp_pos_k", "nsa_cmp_w1_k", "nsa_cmp_w2_k"), ("nsa_cmp_pos_v", "nsa_cmp_w1_v", "nsa_cmp_w2_v"))):
            w1 = w1b[kind]
            w2 = w2b[kind]
            for l4 in range(4):
                P.dma("pool", w1[:, l4 * 8:(l4 + 1) * 8, :], W[w1n][l4 * 1024:(l4 + 1) * 1024, :].rearrange("(l p) h -> p l h", p=128), wr=[w1])
            P.dma("pool", w2[:], W[w2n][:].rearrange("(c p) d -> p c d", p=128), wr=[w2])
            P.dma("sp", posf[:], W[posn][:], wr=[posf])
            P.op("pe", lambda e: e.transpose(out=pc[:, 0:32], in_=posf[:], identity=c["identf"][0:32, 0:32]), rd=[posf, c["identf"]], wr=[pc])
            P.op("act", lambda e: e.copy(out=posT[:], in_=pc[:, 0:32]), rd=[pc], wr=[posT])
            for hc in range(2):
                for l in range(32):
                    P.op("pe", lambda e, hc=hc, l=l, w1=w1: e.matmul(pc[:, 32 + hc:33 + hc], lhsT=w1[:, l, hc * 128:(hc + 1) * 128], rhs=posT[:, l:l + 1],
                                                                    start=(l == 0), stop=(l == 31)), rd=[w1, posT], wr=[pc])
            P.op("act", lambda e: e.copy(out=c1[:], in_=pc[:, 32:34]), rd=[pc], wr=[c1])
            for g in range(2):
                cf = cfull[it % 2]
                it += 1
                for q4 in range(4):
                    P.dma("sp", cf[:, q4 * 2048:(q4 + 1) * 2048], W["cTf"][:, kind * 2 + g, q4 * 2048:(q4 + 1) * 2048], rd=[W["cTf"]], wr=[cf])
                for hc in range(2):
                    p = ph[hc]
                    for l in range(32):
                        P.op("pe", lambda e, p=p, hc=hc, l=l, w1=w1, cf=cf: e.matmul(p[:, 0:511], lhsT=w1[:, l, hc * 128:(hc + 1) * 128],
                                                                                    rhs=cf[:, l:l + 8161:16], start=(l == 0), stop=(l == 31)),
                             rd=[w1, cf], wr=[p])
                    P.op("act", lambda e, p=p, hc=hc: e.activation(out=hid[:, hc, 0:511], in_=p[:, 0:511], func=AF.Silu, bias=c1[:, hc:hc + 1]),
                         rd=[p, c1], wr=[hid])
                for nt in range(4):
                    p2 = po[nt % 2]
                    for hc in range(2):
                        P.op("pe", lambda e, p2=p2, hc=hc, nt=nt, w2=w2: e.matmul(p2[:], lhsT=hid[:, hc, nt * 128:(nt + 1) * 128], rhs=w2[:, hc, :],
                                                                                 start=(hc == 0), stop=(hc == 1)), rd=[hid, w2], wr=[p2])
                    if kind == 0:
                        hn.run(p2[:].unsqueeze(1), p2, 1, gk0, hb[:], hb, cos[:, nt, :], sin[:, nt, :])
                        transposes(k, c, hb, hb, 1, ptr, lambda i, g=g, nt=nt: kcmpT[:, g, nt * 128:(nt + 1) * 128], kcmpT, eng="act")
                    else:
                        P.op("act", lambda e, p2=p2, g=g, nt=nt: e.copy(out=vcmp[:, g, nt, 0:128], in_=p2[:]), rd=[p2], wr=[vcmp])


def nsa_phase(k, c, W, kcmpT, vcmp, mix):
    P = k.P
    with scope(k) as es:
        qT = k.sb(es, "nqT", [128, 8, NTOK], BF16)
        gates = k.sb(es, "ngates", [128, NT, 24], F32)
        TQ = k.sb(es, "nTQ", [128, NTOK], F32)
        tqc = k.sb(es, "ntqc", [128, NT], F32)
        P.dma("sp", qT[:], W["qT"][:], wr=[qT])
        P.dma("sp", gates[:], W["gates"][:], wr=[gates])
        P.dma("sp", TQ[:], W["tokrow"][:].partition_broadcast(128), wr=[TQ])
        P.dma("sp", tqc[:], W["tokcol"][:], wr=[tqc])
        E = k.sb(es, "nE", [128, S], BF16)
        P.op("pool", lambda e: e.memset(E[:], 1.0), wr=[E])
        P.op("pool", lambda e: e.affine_select(out=E[:], in_=E[:], pattern=[[1, S]], compare_op=ALU.is_ge, fill=0.0, base=0, channel_multiplier=-64),
             rd=[E], wr=[E])
        P.op("pool", lambda e: e.affine_select(out=E[:], in_=E[:], pattern=[[-1, S]], compare_op=ALU.is_ge, fill=0.0, base=63, channel_multiplier=64),
             rd=[E], wr=[E])
        ci = k.sb(es, "nci", [128, 128], I32)
        KI = k.sb(es, "nKI", [128, 64], F32)
        KI5 = k.sb(es, "nKI5", [128, 64], F32)
        CL = k.sb(es, "nCL", [128, 4], F32)
        N16 = k.sb(es, "nN16", [128, 4], F32)
        N16b = k.sb(es, "nN16b", [128, 4], F32)
        M64 = k.sb(es, "nM64", [128, 128], F32)
        M64b = k.sb(es, "nM64b", [128, 128], F32)
        m0 = k.sb(es, "nm0", [128, 128], F32)
        P.op("pool", lambda e: e.iota(ci[:, 0:64], pattern=[[128, 64]], base=0, channel_multiplier=1), wr=[ci])
        P.op("dve", lambda e: e.tensor_copy(out=KI[:], in_=ci[:, 0:64]), rd=[ci], wr=[KI])
        P.op("dve", lambda e: e.tensor_scalar(out=KI5[:], in0=KI[:], scalar1=512.0, scalar2=None, op0=ALU.add), rd=[KI], wr=[KI5])
        P.op("pool", lambda e: e.iota(ci[:, 0:4], pattern=[[2048, 4]], base=0, channel_multiplier=16), rd=[KI], wr=[ci])
        P.op("dve", lambda e: e.tensor_copy(out=N16[:], in_=ci[:, 0:4]), rd=[ci], wr=[N16])
        P.op("dve", lambda e: e.tensor_scalar(out=CL[:], in0=N16[:], scalar1=31.0, scalar2=None, op0=ALU.add), rd=[N16], wr=[CL])
        P.op("dve", lambda e: e.tensor_scalar(out=N16b[:], in0=N16[:], scalar1=32.0, scalar2=None, op0=ALU.add), rd=[N16], wr=[N16b])
        P.op("pool", lambda e: e.iota(ci[:], pattern=[[64, 128]], base=0, channel_multiplier=0), rd=[N16], wr=[ci])
        P.op("dve", lambda e: e.tensor_copy(out=M64[:], in_=ci[:]), rd=[ci], wr=[M64])
        P.op("dve", lambda e: e.tensor_scalar(out=M64b[:], in0=M64[:], scalar1=64.0, scalar2=None, op0=ALU.add), rd=[M64], wr=[M64b])
        P.op("dve", lambda e: e.memset(m0[:], 0.0), wr=[m0])
        P.op("dve", lambda e: e.memset(m0[:, 0:1], 1.0), wr=[m0])
        ova = k.sb(es, "nova", [128, 128], F32)
        ovb = k.sb(es, "novb", [128, 128], F32)
        for g in range(2):
            P.op("dve", lambda e, g=g: e.memset(vcmp[:, g, :, 128:129], 1.0), wr=[vcmp])
        for nt in range(4):
            P.op("dve", lambda e, nt=nt: e.tensor_scalar(out=ova[:], in0=M64b[:], scalar1=N16b[:, nt:nt + 1], scalar2=None, op0=ALU.min), rd=[M64b, N16b], wr=[ova])
            P.op("dve", lambda e, nt=nt: e.tensor_scalar(out=ovb[:], in0=M64[:], scalar1=N16[:, nt:nt + 1], scalar2=None, op0=ALU.max), rd=[M64, N16], wr=[ovb])
            P.op("dve", lambda e: e.tensor_tensor(out=ova[:], in0=ova[:], in1=ovb[:], op=ALU.subtract), rd=[ova, ovb], wr=[ova])
            for g in range(2):
                P.op("dve", lambda e, nt=nt, g=g: e.tensor_scalar(out=vcmp[:, g, nt, 129:257], in0=ova[:], scalar1=0.0, scalar2=1.0 / 16, op0=ALU.max, op1=ALU.mult),
                     rd=[ova], wr=[vcmp])
        ksT = k.sb(es, "nksT", [128, S], BF16)
        kwT = k.sb(es, "nkwT", [128, S], BF16)
        vs = k.sb(es, "nvs", [128, 64, 132], BF16)
        vw = k.sb(es, "nvw", [128, 64, 132], BF16)
        cb = [k.sb(es, "ncb%d" % i, [128, 128], BF16) for i in range(4)]
        tcb = [k.sb(es, "ntcb%d" % i, [128, 128], BF16) for i in range(8)]
        wb = [k.sb(es, "nwb%d" % i, [128, 128], BF16) for i in range(12)]
        m2 = k.sb(es, "nm2", [128, 128], BF16)
        PT = [k.sb(es, "nPT%d" % i, [128, 512], BF16) for i in range(3)]
        oacc = k.sb(es, "noacc", [128, 4, 128], F32)
        imp = k.sb(es, "nimp", [128, 128], F32)
        causal = k.sb(es, "ncausal", [128, 128], F32)
        fbc = k.sb(es, "nfbc", [128, 128], F32)
        tmp = k.sb(es, "ntmp", [128, 128], F32)
        score = k.sb(es, "nscore", [128, 128], F32)
        sc2 = k.sb(es, "nsc2", [128, 128], F32)
        m8 = k.sb(es, "nm8", [128, 16], F32)
        selb = k.sb(es, "nselb", [128, 128], BF16)
        selbT = k.sb(es, "nselbT", [128, 128], BF16)
        rs = k.sb(es, "nrs", [128, 8], F32)
        pS = [k.ps(es, "npS%d" % i, [128, 512], F32) for i in range(2)]
        pO = [k.ps(es, "npO%d" % i, [128, 512], F32) for i in range(4)]
        pT_ = k.ps(es, "npT", [128, 8, 128], BF16)
        ident = c["ident"]
        st = {"s": 0, "p": 0}

        def bc4(t):
            return t[:].unsqueeze(1).to_broadcast([128, 4, 128])

        def score_tile(lhsT_ap, lhsT_tile, q3, biases):
            ps_ = pS[st["s"] % 2]
            st["s"] += 1
            pt = PT[st["p"] % 3]
            st["p"] += 1
            nb = len(biases)
            P.op("pe", lambda e: e.matmul(ps_[:].rearrange("p (h q) -> p h q", h=4), lhsT=lhsT_ap, rhs=q3, start=True, stop=(nb == 0)),
                 rd=[lhsT_tile, qT], wr=[ps_])
            for i, (la, lt, bt) in enumerate(biases):
                P.op("pe", lambda e, la=la, bt=bt, i=i: e.matmul(ps_[:].rearrange("p (h q) -> p h q", h=4), lhsT=la, rhs=bc4(bt), start=False, stop=(i == nb - 1)),
                     rd=[lt, bt], wr=[ps_])
            P.op("act", lambda e: e.activation(out=pt[:], in_=ps_[:], func=AF.Exp, scale=SCALE), rd=[ps_], wr=[pt])
            return pt

        def pv(pt, v_ap, v_tile, ncol, first, last):
            for h in range(4):
                P.op("pe", lambda e, h=h: e.matmul(pO[h][:, 0:ncol], lhsT=pt[:, h * 128:(h + 1) * 128], rhs=v_ap, start=first, stop=last),
                     rd=[pt, v_tile], wr=[pO[h]])

        def finish(j, g, br, first):
            for h in range(4):
                r1 = rs[:, h:h + 1]
                r2 = rs[:, 4 + h:5 + h]
                gcol = gates[:, j, (g * 4 + h) * 3 + br:(g * 4 + h) * 3 + br + 1]
                P.op("dve", lambda e, h=h, r1=r1: e.tensor_scalar(out=r1, in0=pO[h][:, 128:129], scalar1=1e-30, scalar2=None, op0=ALU.max), rd=[pO[h]], wr=[rs])
                P.op("dve", lambda e, r1=r1: e.reciprocal(out=r1, in_=r1), rd=[rs], wr=[rs])
                P.op("dve", lambda e, r1=r1, r2=r2, gcol=gcol: e.tensor_tensor(out=r2, in0=r1, in1=gcol, op=ALU.mult), rd=[rs, gates], wr=[rs])
                if first:
                    P.op("dve", lambda e, h=h, r2=r2: e.tensor_scalar(out=oacc[:, h, :], in0=pO[h][:, 0:128], scalar1=r2, scalar2=None, op0=ALU.mult),
                         rd=[pO[h], rs], wr=[oacc])
                else:
                    P.op("dve", lambda e, h=h, r2=r2: e.scalar_tensor_tensor(out=oacc[:, h, :], in0=pO[h][:, 0:128], scalar=r2, in1=oacc[:, h, :],
                                                                            op0=ALU.mult, op1=ALU.add), rd=[pO[h], rs, oacc], wr=[oacc])

        for g in range(2):
            for q4 in range(4):
                sl = slice(q4 * 2048, (q4 + 1) * 2048)
                P.dma("sp", ksT[:, sl], W["kTf"][:, g, sl], rd=[W["kTf"]], wr=[ksT])
                P.dma("sp", kwT[:, sl], W["kTf"][:, 2 + g, sl], rd=[W["kTf"]], wr=[kwT])
            P.op("pool", lambda e: e.memset(vs[:, :, 128:129], 1.0), wr=[vs])
            P.op("pool", lambda e: e.memset(vw[:, :, 128:129], 1.0), wr=[vw])
            for q4 in range(4):
                sl = slice(q4 * 16, (q4 + 1) * 16)
                P.dma("sp", vs[:, sl, 0:128], W["vf"][:, g, sl, :], rd=[W["vf"]], wr=[vs])
                P.dma("sp", vw[:, sl, 0:128], W["vf"][:, 2 + g, sl, :], rd=[W["vf"]], wr=[vw])
            for j in range(NT):
                q3 = qT[:, 4 * g:4 * g + 4, j * 128:(j + 1) * 128]
                tq = TQ[:, j * 128:(j + 1) * 128]
                tcol = tqc[:, j:j + 1]
                for nt in range(4):
                    P.op("dve", lambda e, nt=nt, tq=tq: e.tensor_scalar(out=cb[nt][:], in0=tq, scalar1=CL[:, nt:nt + 1], scalar2=NEG, op0=ALU.is_lt, op1=ALU.mult),
                         rd=[TQ, CL], wr=[cb[nt]])
                for o in range(8):
                    kt = 8 * j + o
                    P.op("dve", lambda e, o=o, kt=kt, tq=tq: e.tensor_scalar(out=tcb[o][:], in0=tq, scalar1=KI[:, kt:kt + 1], scalar2=NEG, op0=ALU.is_lt, op1=ALU.mult),
                         rd=[TQ, KI], wr=[tcb[o]])
                wkts = [kt for kt in range(8 * j - 4, 8 * j + 8) if kt >= 0]
                for i, kt in enumerate(wkts):
                    if kt >= 8 * j:
                        P.op("dve", lambda e, kt=kt, tq=tq: e.tensor_scalar(out=m2[:], in0=tq, scalar1=KI5[:, kt:kt + 1], scalar2=NEG, op0=ALU.is_ge, op1=ALU.mult),
                             rd=[TQ, KI5], wr=[m2])
                        tc_ = tcb[kt - 8 * j]
                        P.op("pool", lambda e, i=i, tc_=tc_: e.tensor_tensor(out=wb[i][:], in0=m2[:], in1=tc_[:], op=ALU.add),
                             rd=[m2, tc_], wr=[wb[i]])
                    else:
                        P.op("dve", lambda e, i=i, kt=kt, tq=tq: e.tensor_scalar(out=wb[i][:], in0=tq, scalar1=KI5[:, kt:kt + 1], scalar2=NEG, op0=ALU.is_ge, op1=ALU.mult),
                             rd=[TQ, KI5], wr=[wb[i]])
                P.op("dve", lambda e, tcol=tcol: e.tensor_scalar(out=causal[:], in0=M64[:], scalar1=tcol, scalar2=None, op0=ALU.is_le), rd=[M64, tqc], wr=[causal])
                P.op("dve", lambda e, tcol=tcol: e.tensor_scalar(out=tmp[:], in0=M64[:], scalar1=128.0, scalar2=tcol, op0=ALU.add, op1=ALU.is_gt), rd=[M64, tqc], wr=[tmp])
                P.op("dve", lambda e: e.tensor_tensor(out=tmp[:], in0=tmp[:], in1=m0[:], op=ALU.max), rd=[tmp, m0], wr=[tmp])
                P.op("dve", lambda e: e.tensor_tensor(out=tmp[:], in0=tmp[:], in1=causal[:], op=ALU.mult), rd=[tmp, causal], wr=[tmp])
                P.op("dve", lambda e: e.tensor_scalar(out=fbc[:], in0=causal[:], scalar1=-1.0, scalar2=1e4, op0=ALU.add, op1=ALU.mult), rd=[causal], wr=[fbc])
                P.op("dve", lambda e: e.scalar_tensor_tensor(out=fbc[:], in0=tmp[:], scalar=1e4, in1=fbc[:], op0=ALU.mult, op1=ALU.add), rd=[tmp, fbc], wr=[fbc])
                for nt in range(4):
                    pt = score_tile(kcmpT[:, g, nt * 128:(nt + 1) * 128], kcmpT, q3, [(ident[:], ident, cb[nt])])
                    pv(pt, vcmp[:, g, nt, 0:257], vcmp, 257, nt == 0, nt == 3)
                finish(j, g, 0, True)
                for h in range(4):
                    r1 = rs[:, h:h + 1]
                    if h == 0:
                        P.op("dve", lambda e, r1=r1: e.tensor_scalar(out=imp[:], in0=pO[0][:, 129:257], scalar1=r1, scalar2=None, op0=ALU.mult),
                             rd=[pO[0], rs], wr=[imp])
                    else:
                        P.op("dve", lambda e, h=h, r1=r1: e.scalar_tensor_tensor(out=imp[:], in0=pO[h][:, 129:257], scalar=r1, in1=imp[:],
                                                                                op0=ALU.mult, op1=ALU.add), rd=[pO[h], rs, imp], wr=[imp])
                P.op("dve", lambda e: e.tensor_tensor(out=score[:], in0=imp[:], in1=causal[:], op=ALU.mult), rd=[imp, causal], wr=[score])
                P.op("dve", lambda e: e.tensor_tensor(out=score[:], in0=score[:], in1=fbc[:], op=ALU.add), rd=[score, fbc], wr=[score])
                P.op("dve", lambda e: e.max(out=m8[:, 0:8], in_=score[:]), rd=[score], wr=[m8])
                P.op("dve", lambda e: e.match_replace(out=sc2[:], in_to_replace=m8[:, 0:8], in_values=score[:], imm_value=-1e30), rd=[score, m8], wr=[sc2])
                P.op("dve", lambda e: e.max(out=m8[:, 8:16], in_=sc2[:]), rd=[sc2], wr=[m8])
                P.op("dve", lambda e: e.scalar_tensor_tensor(out=sc2[:], in0=score[:], scalar=m8[:, 15:16], in1=causal[:], op0=ALU.is_ge, op1=ALU.mult),
                     rd=[score, m8, causal], wr=[sc2])
                P.op("dve", lambda e: e.tensor_scalar(out=selb[:], in0=sc2[:], scalar1=-1.0, scalar2=-NEG, op0=ALU.add, op1=ALU.mult), rd=[sc2], wr=[selb])
                P.op("pe", lambda e: e.transpose(out=pT_[:, 0, :], in_=selb[:], identity=ident[:]), rd=[selb, ident], wr=[pT_])
                P.op("act", lambda e: e.copy(out=selbT[:], in_=pT_[:, 0, :]), rd=[pT_], wr=[selbT])
                nk = 8 * j + 8
                for kt in range(nk):
                    bl = [(E[:, kt * 128:(kt + 1) * 128], E, selbT)]
                    if kt >= 8 * j:
                        bl.append((ident[:], ident, tcb[kt - 8 * j]))
                    pt = score_tile(ksT[:, kt * 128:(kt + 1) * 128], ksT, q3, bl)
                    pv(pt, vs[:, kt, 0:129], vs, 129, kt == 0, kt == nk - 1)
                finish(j, g, 1, False)
                for i, kt in enumerate(wkts):
                    pt = score_tile(kwT[:, kt * 128:(kt + 1) * 128], kwT, q3, [(ident[:], ident, wb[i])])
                    pv(pt, vw[:, kt, 0:129], vw, 129, i == 0, i == len(wkts) - 1)
                finish(j, g, 2, False)
                P.op("act", lambda e, j=j, g=g: e.copy(out=mix[:, j, 512 + g * 512:1024 + g * 512], in_=oacc[:].rearrange("p h n -> p (h n)")),
                     rd=[oacc], wr=[mix])


def wout_phase(k, c, W, mix, x2):
    P = k.P
    with scope(k) as es:
        om = k.sb(es, "wom", [128, NT, 512], F32)
        P.dma("sp", om[:], W["omem"][:], wr=[om])
        for j in range(NT):
            P.op("act", lambda e, j=j: e.copy(out=mix[:, j, 1536:2048], in_=om[:, j, :]), rd=[om], wr=[mix])
        mixT = k.sb(es, "mixT", [128, KC, NTOK], BF16)
        pts = [k.ps(es, "wpt%d" % i, [128, 8, 128], BF16) for i in range(2)]
        it = 0
        for j in range(NT):
            for half in range(2):
                pt = pts[it % 2]
                it += 1
                for i in range(8):
                    kc = half * 8 + i
                    P.op("pe", lambda e, pt=pt, i=i, kc=kc, j=j: e.transpose(out=pt[:, i, :], in_=mix[:, j, kc * 128:(kc + 1) * 128], identity=c["ident"][:]),
                         rd=[mix, c["ident"]], wr=[pt])
                if half == 0:
                    P.op("act", lambda e, pt=pt, j=j: e.copy(out=mixT[:, 0:8, j * 128:(j + 1) * 128], in_=pt[:]), rd=[pt], wr=[mixT])
                else:
                    P.op("dve", lambda e, pt=pt, j=j: e.tensor_copy(out=mixT[:, 8:16, j * 128:(j + 1) * 128], in_=pt[:]), rd=[pt], wr=[mixT])
        wo = [k.sb(es, "wo%d" % i, [128, KC, 512], BF16) for i in range(2)]
        xs = [k.sb(es, "wxs%d" % i, [128, 512], F32) for i in range(3)]
        ys = [k.sb(es, "wys%d" % i, [128, 512], F32) for i in range(3)]
        pp = [k.ps(es, "wpp%d" % i, [128, 512], F32) for i in range(2)]
        it = 0
        for dg in range(4):
            b = wo[dg % 2]
            P.dma("pool", b[:], W["w_out"][:, dg * 512:(dg + 1) * 512].rearrange("(k p) f -> p k f", p=128), rd=[W["w_out"]], wr=[b])
            for j in range(NT):
                p = pp[it % 2]
                x_ = xs[it % 3]
                y_ = ys[it % 3]
                it += 1
                P.dma("sp", x_[:], W["x1"][j * 128:(j + 1) * 128, dg * 512:(dg + 1) * 512], rd=[W["x1"]], wr=[x_])
                for kk in range(KC):
                    P.op("pe", lambda e, p=p, b=b, kk=kk, j=j: e.matmul(p[:], lhsT=mixT[:, kk, j * 128:(j + 1) * 128], rhs=b[:, kk, :],
                                                                       start=(kk == 0), stop=(kk == KC - 1)), rd=[mixT, b], wr=[p])
                P.op("dve", lambda e, p=p, x_=x_, y_=y_: e.tensor_tensor(out=y_[:], in0=p[:], in1=x_[:], op=ALU.add), rd=[p, x_], wr=[y_])
                P.dma("act", x2[j * 128:(j + 1) * 128, dg * 512:(dg + 1) * 512], y_[:], rd=[y_], wr=[x2])


def final_norm_phase(k, c, src, gain_dram, dst):
    P = k.P
    with scope(k) as es:
        gb = k.sb(es, "fgb", [128, D], F32)
        P.dma("sp", gb[:], gain_dram[:].partition_broadcast(128), wr=[gb])
        xs = [k.sb(es, "fx%d" % i, [128, D], F32) for i in range(2)]
        ys = [k.sb(es, "fy%d" % i, [128, D], F32) for i in range(2)]
        junk = k.sb(es, "fjunk", [128, D], BF16)
        st = k.sb(es, "fst", [128, 3 * NT], F32)
        P.op("dve", lambda e: e.memset(st[:], 0.0), wr=[st])
        for tt in range(NT):
            x = xs[tt % 2]
            y = ys[tt % 2]
            P.dma("sp", x[:], src[tt * 128:(tt + 1) * 128, :], rd=[src], wr=[x])
            P.op("act", lambda e, x=x, tt=tt: e.activation(out=junk[:], in_=x[:], func=AF.Square, accum_out=st[:, tt:tt + 1]), rd=[x], wr=[junk, st])
            P.op("act", lambda e, tt=tt: e.activation(out=st[:, NT + tt:NT + tt + 1], in_=st[:, tt:tt + 1], func=AF.Sqrt, scale=1.0 / D, bias=EPS),
                 rd=[st], wr=[st])
            P.op("dve", lambda e, tt=tt: e.reciprocal(out=st[:, 2 * NT + tt:2 * NT + tt + 1], in_=st[:, NT + tt:NT + tt + 1]), rd=[st], wr=[st])
            P.op("dve", lambda e, x=x, y=y, tt=tt: e.scalar_tensor_tensor(out=y[:], in0=x[:], scalar=st[:, 2 * NT + tt:2 * NT + tt + 1], in1=gb[:],
                                                                         op0=ALU.mult, op1=ALU.mult), rd=[x, st, gb], wr=[y])
            P.dma("act", dst[tt * 128:(tt + 1) * 128, :], y[:], rd=[y], wr=[dst])


B_IN = {
    "x1": ([NTOK, D], F32), "qT": ([128, 8, NTOK], BF16), "gates": ([128, NT, 24], F32), "gqT": ([128, 2, NTOK], BF16), "gkT": ([128, 2, NTOK], BF16),
    "gv": ([128, NT, 512], BF16), "sgr": ([128, NT, 512], F32), "omem": ([128, NT, 512], F32),
    "tokrow": ([1, NTOK], F32), "tokcol": ([128, NT], F32), "onehot": ([1, 8], F32),
    "kTf": ([128, 4, S], BF16), "cTf": ([128, 4, S], BF16), "vf": ([128, 4, 64, 128], BF16), "gLf": ([64, 2, 128, 128], F32), "gDf": ([128, 64, 2], F32),
    "pos_cmp": ([128, 4], I32), "rope_inv": ([1, 64], F32),
    "gla_o_norm": ([1, 128], F32), "nsa_k_norm": ([3, 128], F32),
    "nsa_cmp_pos_k": ([32, 128], F32), "nsa_cmp_w1_k": ([4096, 256], F32), "nsa_cmp_w2_k": ([256, 128], F32),
    "nsa_cmp_pos_v": ([32, 128], F32), "nsa_cmp_w1_v": ([4096, 256], F32), "nsa_cmp_w2_v": ([256, 128], F32),
    "w_out": ([D, D], F32), "ffn2_norm": ([1, D], F32), "ffn2_w_gate": ([D, DFF], F32), "ffn2_w_up": ([D, DFF], F32), "ffn2_w_down": ([DFF, D], F32),
    "final_norm": ([1, D], F32),
}


class LazyIn(dict):
    def __init__(self, k, spec):
        super().__init__()
        self.k = k
        self.spec = spec

    def __missing__(self, n):
        sh, dt = self.spec[n]
        t = self.k.dram_in(n, sh, dt)
        self[n] = t
        return t


def build_B(stop_after=None, lazy=False, info=None):
    nc = bass.Bass("TRN2", target_bir_lowering=False)
    with ExitStack() as es:
        k = K(nc, es)
        if lazy:
            W = LazyIn(k, B_IN)
        else:
            W = {n: k.dram_in(n, sh, dt) for n, (sh, dt) in B_IN.items()}
        if info is not None:
            info["W"] = W
        out = k.dram_out("out", [NTOK, D], F32)
        x2 = k.dram_tmp("x2s", [NTOK, D], F32)
        x3 = k.dram_tmp("x3s", [NTOK, D], F32)
        c = make_consts(k, es)
        k.P.barrier()
        dbg = None
        with scope(k) as es2:
            mix = k.sb(es2, "mix", [128, NT, D], BF16)
            k.P.op("pool", lambda e: e.memset(mix[:], 0.0), wr=[mix])
            gla_phase(k, c, W, mix)
            if stop_after != "gla":
                with scope(k) as es3:
                    kcmpT = k.sb(es3, "kcmpT", [128, 2, 512], BF16)
                    vcmp = k.sb(es3, "vcmp", [128, 2, 4, 260], BF16)
                    k.P.op("pool", lambda e: e.memset(vcmp[:], 0.0), wr=[vcmp])
                    compress_phase(k, c, W, kcmpT, vcmp)
                    nsa_phase(k, c, W, kcmpT, vcmp, mix)
            if stop_after in ("gla", "nsa"):
                dbg = k.dram_out("mixdbg", [128, NT, D], BF16)
                k.P.dma("sp", dbg[:], mix[:], rd=[mix], wr=[dbg])
            else:
                wout_phase(k, c, W, mix, x2)
        if stop_after is None:
            ffn_phase(k, c, x2, x3, W["ffn2_norm"], W["ffn2_w_gate"], W["ffn2_w_up"], W["ffn2_w_down"], "2")
            final_norm_phase(k, c, x3, W["final_norm"], out)
        k.P.barrier()
        k.P.emit()
    return nc


def _shard_tok(a, cc):
    return np.ascontiguousarray(a.reshape(8, 8, 128, *a.shape[1:])[:, cc].reshape(NTOK, *a.shape[1:]))


def _rope_inv():
    return (np.float32(10000.0) ** (-(np.arange(64, dtype=np.float32) / np.float32(64)))).astype(np.float32)[None]


def make_A_inputs(inp):
    ins = []
    for cc in range(NCORES):
        m = {"x": _shard_tok(inp["x"][0], cc),
             "pos": np.ascontiguousarray(_shard_tok(inp["positions"][0], cc).reshape(NT, 128).T).astype(np.int32),
             "rope_inv": _rope_inv(), "mem": np.ascontiguousarray(inp["mem"][0])}
        for n, (sh, _) in A_IN.items():
            if n not in m:
                m[n] = np.ascontiguousarray(np.asarray(inp[n])[0]).reshape(sh)
        ins.append(m)
    return ins


def make_B_inputs(inp, resA):
    kTf = np.empty((128, 4, 64, 128), dtype=resA[0]["kT"].dtype)
    cTf = np.empty((128, 4, 64, 128), dtype=resA[0]["cT"].dtype)
    vf = np.empty((128, 4, 64, 128), dtype=resA[0]["vtm"].dtype)
    gLf = np.empty((64, 2, 128, 128), np.float32)
    gDf = np.empty((128, 64, 2), np.float32)
    for cc in range(NCORES):
        r = resA[cc]
        kTf[:, :, cc::8, :] = np.asarray(r["kT"]).reshape(128, 4, NT, 128)
        cTf[:, :, cc::8, :] = np.asarray(r["cT"]).reshape(128, 4, NT, 128)
        vf[:, :, cc::8, :] = np.asarray(r["vtm"])
        gLf[cc::8] = np.asarray(r["gL"])
        gDf[:, cc::8, :] = np.asarray(r["gD"])
    kTf = kTf.reshape(128, 4, S)
    cTf = cTf.reshape(128, 4, S)
    pos = np.asarray(inp["positions"])[0]
    pc = np.zeros(512, np.int32)
    pc[:511] = pos[31::16][:511]
    pos_cmp = np.ascontiguousarray(pc.reshape(4, 128).T)
    ins = []
    for cc in range(NCORES):
        r = resA[cc]
        tok = _shard_tok(np.arange(S, dtype=np.float32), cc)
        oh = np.zeros((1, 8), np.float32)
        oh[0, cc] = 1.0
        m = {n: np.asarray(r[n]) for n in ("x1", "qT", "gates", "gqT", "gkT", "gv", "sgr", "omem")}
        m.update({"tokrow": tok[None, :].copy(), "tokcol": np.ascontiguousarray(tok.reshape(NT, 128).T), "onehot": oh,
                  "kTf": kTf, "cTf": cTf, "vf": vf, "gLf": gLf, "gDf": gDf, "pos_cmp": pos_cmp, "rope_inv": _rope_inv()})
        for n, (sh, _) in B_IN.items():
            if n not in m:
                m[n] = np.ascontiguousarray(np.asarray(inp[n])[0]).reshape(sh)
        ins.append(m)
    return ins


def kernel(**inp):
    inp = {k_: np.asarray(v) for k_, v in inp.items()}
    ncA = build_A()
    resA = run_bass_kernel_spmd(ncA, make_A_inputs(inp), core_ids=list(range(NCORES))).results
    ncB = build_B()
    resB = run_bass_kernel_spmd(ncB, make_B_inputs(inp, resA), core_ids=list(range(NCORES))).results
    out = np.empty((8, 8, 128, D), np.float32)
    for cc in range(NCORES):
        out[:, cc] = np.asarray(resB[cc]["out"]).reshape(NT, 128, D)
    return out.reshape(1, S, D)
```
